# Optimizing a Trainium2 kernel written in Bass

```python
import jax, jax.numpy as jnp
from jax import lax
import numpy as np

D_MODEL = 1024
BATCH = 8
SEQ = 4096
DEPTH = 1

ATT_HEADS = 8
ATT_HEAD_DIM = 128
ATT_WIDTH = ATT_HEADS * ATT_HEAD_DIM
MOBA_BLOCK = 256
MOBA_TOPK = 3
Q_CHUNK = 128
ROPE_THETA = 500000.0
ROPE_DIM = ATT_HEAD_DIM // 4

GDN_HEADS = 8
GDN_KDIM = 128
GDN_VDIM = 128
GDN_KWIDTH = GDN_HEADS * GDN_KDIM
GDN_VWIDTH = GDN_HEADS * GDN_VDIM
CONV_WIDTH = 4
GDN_CHUNK = 64

EPS = 1e-6
NEG = -1e30

IN_SPLITS = [ATT_WIDTH, ATT_WIDTH, ATT_WIDTH, ATT_WIDTH,
             GDN_KWIDTH, GDN_KWIDTH, GDN_VWIDTH, GDN_VWIDTH,
             GDN_HEADS, GDN_HEADS,
             D_MODEL, D_MODEL]
IN_COLS = int(sum(IN_SPLITS))
IN_OFFSETS = [int(o) for o in np.cumsum(IN_SPLITS)[:-1]]

kernel_name = "moba_gdn_gated_hybrid_block"


def rmsnorm(x, w):
    xf = x.astype(jnp.float32)
    y = xf * lax.rsqrt(jnp.mean(xf * xf, axis=-1, keepdims=True) + EPS)
    return (y * w.astype(jnp.float32)).astype(x.dtype)


def l2norm(x):
    return x * lax.rsqrt(jnp.sum(x * x, axis=-1, keepdims=True) + EPS)


def partial_rope(x, pos):
    half = ROPE_DIM // 2
    inv_freq = jnp.power(ROPE_THETA, -jnp.arange(half, dtype=jnp.float32) * (2.0 / ROPE_DIM))
    ang = pos.astype(jnp.float32)[:, None] * inv_freq[None, :]
    cos, sin = jnp.cos(ang), jnp.sin(ang)
    xr = x[..., :ROPE_DIM].astype(jnp.float32)
    x1, x2 = xr[..., :half], xr[..., half:]
    rot = jnp.concatenate([x1 * cos - x2 * sin, x2 * cos + x1 * sin], axis=-1).astype(x.dtype)
    return jnp.concatenate([rot, x[..., ROPE_DIM:]], axis=-1)


def moba_attention(q, k, v):
    bsz, nh, s, dh = q.shape
    nb = -(-s // MOBA_BLOCK)
    pad = nb * MOBA_BLOCK - s
    nc = s // Q_CHUNK
    kk = min(MOBA_TOPK, nb)
    scale = dh ** -0.5
    padw = ((0, 0), (0, 0), (0, pad), (0, 0))
    kblk = jnp.pad(k, padw).reshape(bsz, nh, nb, MOBA_BLOCK, dh)
    vblk = jnp.pad(v, padw).reshape(bsz, nh, nb, MOBA_BLOCK, dh)
    kmean = jnp.mean(kblk.astype(jnp.float32), axis=3)
    qch = q.reshape(bsz, nh, nc, Q_CHUNK, dh)
    head_idx = jnp.arange(nh)[:, None, None]

    def per_batch(args):
        q_b, k_b, v_b, km_b = args

        def per_chunk(c):
            q_c = lax.dynamic_index_in_dim(q_b, c, axis=1, keepdims=False)
            q_pos = c * Q_CHUNK + jnp.arange(Q_CHUNK)
            blk = (c * Q_CHUNK) // MOBA_BLOCK
            gate = jnp.einsum('hqd,hnd->hqn', q_c.astype(jnp.float32), km_b)
            gate = jnp.where((jnp.arange(nb) < blk)[None, None, :], gate, -jnp.inf)
            _, idx = lax.top_k(gate, kk)
            sel_valid = jnp.arange(kk) < blk
            k_sel = k_b[head_idx, idx]
            v_sel = v_b[head_idx, idx]
            s_sel = jnp.einsum('hqd,hqjld->hqjl', q_c, k_sel).astype(jnp.float32) * scale
            s_sel = jnp.where(sel_valid[None, None, :, None], s_sel, NEG)
            k_own = lax.dynamic_index_in_dim(k_b, blk, axis=1, keepdims=False)
            v_own = lax.dynamic_index_in_dim(v_b, blk, axis=1, keepdims=False)
            s_own = jnp.einsum('hqd,hld->hql', q_c, k_own).astype(jnp.float32) * scale
            k_pos = blk * MOBA_BLOCK + jnp.arange(MOBA_BLOCK)
            s_own = jnp.where(k_pos[None, None, :] <= q_pos[None, :, None], s_own, NEG)
            scores = jnp.concatenate([s_sel.reshape(nh, Q_CHUNK, kk * MOBA_BLOCK), s_own], axis=-1)
            p = jax.nn.softmax(scores, axis=-1).astype(v_b.dtype)
            p_sel = p[..., :kk * MOBA_BLOCK].reshape(nh, Q_CHUNK, kk, MOBA_BLOCK)
            p_own = p[..., kk * MOBA_BLOCK:]
            return (jnp.einsum('hqjl,hqjld->hqd', p_sel, v_sel)
                    + jnp.einsum('hql,hld->hqd', p_own, v_own))

        return lax.map(per_chunk, jnp.arange(nc))

    out = lax.map(per_batch, (qch, kblk, vblk, kmean))
    return out.transpose(0, 1, 3, 2, 4).reshape(bsz, s, nh * dh)


def causal_dwconv(x, w):
    width = w.shape[0]
    return lax.conv_general_dilated(x, w[:, None, :], window_strides=(1,), padding=[(width - 1, 0)],
                                    dimension_numbers=('NWC', 'WIO', 'NWC'),
                                    feature_group_count=x.shape[-1])


def gated_deltanet(q, k, v, g, beta):
    bsz, s, nh, dk = q.shape
    dv = v.shape[-1]
    c = GDN_CHUNK
    n = s // c
    q = l2norm(q) * (dk ** -0.5)
    k = l2norm(k)

    def chunks(t):
        return t.reshape(bsz, n, c, nh, -1).transpose(0, 3, 1, 2, 4)

    qc, kc, vc = chunks(q), chunks(k), chunks(v)
    gc = g.reshape(bsz, n, c, nh).transpose(0, 3, 1, 2)
    bc = beta.reshape(bsz, n, c, nh).transpose(0, 3, 1, 2)
    G = jnp.cumsum(gc, axis=-1)
    causal = jnp.tril(jnp.ones((c, c), dtype=bool))
    strict = jnp.tril(jnp.ones((c, c), dtype=bool), k=-1)
    diff = G[..., :, None] - G[..., None, :]
    decay = jnp.where(causal, jnp.exp(jnp.where(causal, diff, 0.0)), 0.0)
    k_beta = kc * bc[..., None]
    m = jnp.where(strict, jnp.einsum('bhnid,bhnjd->bhnij', k_beta, kc) * decay, 0.0)
    a_mat = m + jnp.eye(c, dtype=m.dtype)
    rhs = jnp.concatenate([vc * bc[..., None], k_beta * jnp.exp(G)[..., None]], axis=-1)
    sol = lax.linalg.triangular_solve(a_mat, rhs, left_side=True, lower=True, unit_diagonal=True)
    u, w = sol[..., :dv], sol[..., dv:]
    qk = jnp.where(causal, jnp.einsum('bhnid,bhnjd->bhnij', qc, kc) * decay, 0.0)
    q_dec = qc * jnp.exp(G)[..., None]
    k_dec = kc * jnp.exp(G[..., -1:] - G)[..., None]
    g_last = jnp.exp(G[..., -1])

    def step(state, xs):
        u_i, w_i, q_i, qk_i, k_i, gl_i = xs
        v_new = u_i - jnp.einsum('bhcd,bhde->bhce', w_i, state)
        o_i = jnp.einsum('bhcd,bhde->bhce', q_i, state) + jnp.einsum('bhij,bhje->bhie', qk_i, v_new)
        state = state * gl_i[..., None, None] + jnp.einsum('bhcd,bhce->bhde', k_i, v_new)
        return state, o_i

    xs = tuple(jnp.moveaxis(t, 2, 0) for t in (u, w, q_dec, qk, k_dec, g_last))
    state0 = jnp.zeros((bsz, nh, dk, dv), dtype=jnp.float32)
    _, o = lax.scan(step, state0, xs)
    return o.transpose(1, 0, 3, 2, 4).reshape(bsz, s, nh, dv)


def hybrid_layer(x, pre_norm_w, w_in, conv_w, a_log, dt_bias, gdn_norm_w,
                 w_branch_a, w_branch_b, w_out, post_norm_w):
    bsz, s, _ = x.shape
    f32 = jnp.float32
    h = rmsnorm(x, pre_norm_w)
    proj = jnp.einsum('bsd,de->bse', h, w_in)
    (q_a, k_a, v_a, z_a, q_b, k_b, v_b, z_b,
     beta_logit, decay_logit, gate_a, gate_b) = jnp.split(proj, IN_OFFSETS, axis=-1)

    pos = jnp.arange(s)

    def heads(t):
        return t.reshape(bsz, s, ATT_HEADS, ATT_HEAD_DIM).transpose(0, 2, 1, 3)

    o_a = moba_attention(partial_rope(heads(q_a), pos), partial_rope(heads(k_a), pos), heads(v_a))
    y_a = jnp.einsum('bse,ed->bsd', o_a * jax.nn.silu(z_a), w_branch_a)

    qkv = jax.nn.silu(causal_dwconv(jnp.concatenate([q_b, k_b, v_b], axis=-1), conv_w))
    q_b, k_b, v_b = jnp.split(qkv, [GDN_KWIDTH, 2 * GDN_KWIDTH], axis=-1)
    g = -jnp.exp(a_log.astype(f32)) * jax.nn.softplus(decay_logit.astype(f32) + dt_bias.astype(f32))
    beta = jax.nn.sigmoid(beta_logit.astype(f32))
    o_b = gated_deltanet(q_b.reshape(bsz, s, GDN_HEADS, GDN_KDIM).astype(f32),
                         k_b.reshape(bsz, s, GDN_HEADS, GDN_KDIM).astype(f32),
                         v_b.reshape(bsz, s, GDN_HEADS, GDN_VDIM).astype(f32), g, beta)
    o_b = rmsnorm(o_b, gdn_norm_w).reshape(bsz, s, GDN_VWIDTH).astype(x.dtype)
    y_b = jnp.einsum('bse,ed->bsd', o_b * jax.nn.silu(z_b), w_branch_b)

    merged = jax.nn.sigmoid(gate_a) * y_a + jax.nn.sigmoid(gate_b) * y_b
    out = jnp.einsum('bsd,de->bse', merged, w_out)
    return x + rmsnorm(out, post_norm_w)


def setup_inputs(seed: int = 0) -> dict:
    key = jax.random.key(seed)
    ks = jax.random.split(key, 12)
    conv_ch = 2 * GDN_KWIDTH + GDN_VWIDTH
    x = jax.random.normal(ks[0], (BATCH, SEQ, D_MODEL), jnp.float32)
    pre_norm_w = 1.0 + 0.05 * jax.random.normal(ks[1], (DEPTH, D_MODEL), jnp.float32)
    w_in = jax.random.normal(ks[2], (DEPTH, D_MODEL, IN_COLS), jnp.float32) * D_MODEL ** -0.5
    conv_w = jax.random.normal(ks[3], (DEPTH, CONV_WIDTH, conv_ch), jnp.float32) * CONV_WIDTH ** -0.5
    a_log = jnp.log(jax.random.uniform(ks[4], (DEPTH, GDN_HEADS), jnp.float32, 1.0, 16.0))
    dt = jnp.exp(jax.random.uniform(ks[5], (DEPTH, GDN_HEADS), jnp.float32,
                                    float(np.log(1e-3)), float(np.log(1e-1))))
    dt_bias = dt + jnp.log(-jnp.expm1(-dt))
    gdn_norm_w = 1.0 + 0.05 * jax.random.normal(ks[6], (DEPTH, GDN_VDIM), jnp.float32)
    w_branch_a = jax.random.normal(ks[7], (DEPTH, ATT_WIDTH, D_MODEL), jnp.float32) * ATT_WIDTH ** -0.5
    w_branch_b = jax.random.normal(ks[8], (DEPTH, GDN_VWIDTH, D_MODEL), jnp.float32) * GDN_VWIDTH ** -0.5
    w_out = jax.random.normal(ks[9], (DEPTH, D_MODEL, D_MODEL), jnp.float32) * D_MODEL ** -0.5
    post_norm_w = 1.0 + 0.05 * jax.random.normal(ks[10], (DEPTH, D_MODEL), jnp.float32)
    return {"x": x, "pre_norm_w": pre_norm_w, "w_in": w_in, "conv_w": conv_w,
            "a_log": a_log, "dt_bias": dt_bias, "gdn_norm_w": gdn_norm_w,
            "w_branch_a": w_branch_a, "w_branch_b": w_branch_b, "w_out": w_out,
            "post_norm_w": post_norm_w}


def reference(x, pre_norm_w, w_in, conv_w, a_log, dt_bias, gdn_norm_w,
              w_branch_a, w_branch_b, w_out, post_norm_w):
    for layer in range(DEPTH):
        x = hybrid_layer(x, pre_norm_w[layer], w_in[layer], conv_w[layer], a_log[layer],
                         dt_bias[layer], gdn_norm_w[layer], w_branch_a[layer],
                         w_branch_b[layer], w_out[layer], post_norm_w[layer])
    return x
```

```python
import contextlib
import numpy as np
import concourse.bass as bass
import concourse.mybir as mybir
from concourse.bass_utils import run_bass_kernel_spmd

F32 = mybir.dt.float32
BF16 = mybir.dt.bfloat16
AF = mybir.ActivationFunctionType
ALU = mybir.AluOpType
AX = mybir.AxisListType

S = 4096
D = 1024
NH = 8
HD = 128
KC = 8
NT = S // 128
NB = S // 512
EPS = 1e-6
NEG = -30000.0
ROPE_THETA = 500000.0

DEBUG = {}


class Buf:
    __slots__ = ("name", "w", "r", "excl")

    def __init__(self, name, excl=False):
        self.name = name
        self.w = None
        self.r = {}
        self.excl = excl


class EngQ:
    def __init__(self, name, eng, sem):
        self.name = name
        self.e = eng
        self.sem = sem
        self.count = 0
        self.seen = {}
        self.events = []


class Chan:
    def __init__(self, name, sem):
        self.name = name
        self.sem = sem
        self.count = 0


class KB:
    def __init__(self, nc, stack):
        self.nc = nc
        self.stack = stack
        self.q = {}
        for name, eng in (("pe", nc.tensor), ("act", nc.scalar), ("dve", nc.vector),
                          ("pool", nc.gpsimd), ("sp", nc.sync)):
            sem = stack.enter_context(nc.semaphore("sem_" + name))
            self.q[name] = EngQ(name, eng, sem)
        self.semq = {id(q.sem): q for q in self.q.values()}
        self.chans = {}
        self.nbuf = 0
        self.nops = 0
        self.cut = None

    def chan(self, name):
        if name not in self.chans:
            sem = self.stack.enter_context(self.nc.semaphore("ch_" + name))
            self.chans[name] = Chan(name, sem)
        return self.chans[name]

    def buf(self, name=None):
        self.nbuf += 1
        return Buf(name or ("b%d" % self.nbuf))

    def bufs(self, n, name="b"):
        return [self.buf("%s%d" % (name, i)) for i in range(n)]

    def _wait(self, q, tk):
        if tk is None:
            return
        sem, val = tk
        key = id(sem)
        if q.seen.get(key, 0) >= val:
            return
        owner = self.semq.get(key)
        if owner is q and q.name == "pe":
            return
        if owner is q and val > q.count:
            return
        q.e.wait_ge(sem, val)
        q.seen[key] = val
        q.events.append(("w", key, val))

    def _deps(self, q, reads, writes):
        for b in reads:
            self._wait(q, b.w)
        for b in writes:
            self._wait(q, b.w)
            for key, tk in list(b.r.items()):
                self._wait(q, tk)

    def _mark(self, tk, reads, writes):
        key = id(tk[0])
        for b in reads:
            old = b.r.get(key)
            if old is None or old[1] < tk[1]:
                b.r[key] = tk
        for b in writes:
            b.w = tk
            b.r = {}

    def barrier(self, names=("pe", "act", "dve", "pool", "sp")):
        for n in names:
            q = self.q[n]
            for o in self.q.values():
                if o is not q and o.count:
                    self._wait(q, (o.sem, o.count))
            for c in self.chans.values():
                if c.count:
                    self._wait(q, (c.sem, c.count))

    def op(self, qn, fn, reads=(), writes=(), sig=True):
        if any(b.excl for b in reads):
            writes = list(writes) + [b for b in reads if b.excl]
            reads = [b for b in reads if not b.excl]
        self.nops += 1
        if self.cut is not None and self.nops > self.cut:
            if not sig:
                return None
            sig = True
            fn = lambda e: e.nop()
            reads, writes = (), ()
        q = self.q[qn]
        self._deps(q, reads, writes)
        ins = fn(q.e)
        tk = (q.sem, q.count + 1)
        if sig:
            ins.then_inc(q.sem, 1)
            q.count += 1
            q.events.append(("i", id(q.sem), 1))
        self._mark(tk, reads, writes)
        return ins

    def dma(self, out, in_, reads=(), writes=(), chan=None, qn="sp", **kw):
        self.nops += 1
        if self.cut is not None and self.nops > self.cut:
            return
        q = self.q[qn]
        ch = self.chan(chan) if isinstance(chan, str) else chan
        self._deps(q, reads, writes)
        self._wait(q, (ch.sem, ch.count))
        q.e.dma_start(out=out, in_=in_, **kw).then_inc(ch.sem, 16)
        ch.count += 16
        q.events.append(("i", id(ch.sem), 16))
        tk = (ch.sem, ch.count)
        self._mark(tk, reads, writes)

    def check_deadlock(self):
        sems = {}
        pos = {n: 0 for n in self.q}
        names = {id(q.sem): "sem_" + q.name for q in self.q.values()}
        names.update({id(c.sem): "ch_" + c.name for c in self.chans.values()})
        progress = True
        while progress:
            progress = False
            for n, q in self.q.items():
                ev = q.events
                while pos[n] < len(ev):
                    kind, key, val = ev[pos[n]]
                    if kind == "w":
                        if sems.get(key, 0) < val:
                            break
                    else:
                        sems[key] = sems.get(key, 0) + val
                    pos[n] += 1
                    progress = True
        stuck = {n: (pos[n], len(q.events)) for n, q in self.q.items() if pos[n] < len(q.events)}
        for n in stuck:
            kind, key, val = self.q[n].events[pos[n]]
            print("DEADLOCK: queue", n, "stuck at event", pos[n], "of", len(self.q[n].events), "waiting", names.get(key), ">=", val,
                  "have", sems.get(key, 0))
        return not stuck

    def finish(self, bufs):
        q = self.q["sp"]
        for b in bufs:
            self._wait(q, b.w)
            for tk in list(b.r.values()):
                self._wait(q, tk)


def mm_group(kb, out_ap, pairs, reads, writes):
    n = len(pairs)
    for i, (l, r) in enumerate(pairs):
        kb.op("pe", lambda e, l=l, r=r, i=i: e.matmul(out_ap, lhsT=l, rhs=r, start=(i == 0), stop=(i == n - 1)),
              reads=reads, writes=writes, sig=(i == n - 1))


WM_COLS = 576


def build_program(dbg=None):
    dbg = dbg or {}
    stop_after = dbg.get("stop_after")
    nc = bass.Bass("TRN2", target_bir_lowering=False)

    def din(name, shape, dt=F32):
        return nc.dram_tensor(name, list(shape), dt, kind="ExternalInput").ap()

    x_d = din("x", [S, D])
    pnwT_d = din("pnwT", [128, KC])
    idf_d = din("idf", [128, 128])
    wm_d = din("wm", [NH, 128, KC, WM_COLS])
    tabc_d = din("ropecos", [32, S])
    tabs_d = din("ropesin", [64, S])
    ut_d = din("utmask", [128, 128])
    negt_d = din("negt", [128, 128])
    smask_d = din("smask", [128, 128])
    wg_d = din("wg", [NH, 128, KC, 514])
    cw_d = din("cw", [128, 24, 4])
    dtb_d = din("dtb", [128, NH])
    alog_d = din("alog", [128, NH])
    gnw_d = din("gnw", [128, 1])
    wa_d = din("wa", [128, KC, D])
    wb_d = din("wb", [128, KC, D])
    wo_d = din("wo", [128, KC, D])
    wga_d = din("wga", [128, KC, D])
    wgb_d = din("wgb", [128, KC, D])
    pnw_d = din("pnw", [128, D])
    y_d = nc.dram_tensor("y", [S, D], F32, kind="ExternalOutput").ap()
    hT_d = nc.dram_tensor("hT_scr", [128, KC, S], BF16, kind="Internal").ap()
    oaT_d = nc.dram_tensor("oaT_scr", [NH, 128, S], BF16, kind="Internal").ap()
    obT_d = nc.dram_tensor("obT_scr", [NH, 128, S], BF16, kind="Internal").ap()
    dbg_out = {}
    for name, shape, dt in dbg.get("outs", []):
        dbg_out[name] = nc.dram_tensor(name, list(shape), dt, kind="ExternalOutput").ap()

    with contextlib.ExitStack() as top:
        kb = KB(nc, top)
        kb.cut = dbg.get("cut")

        def sb(st, name, shape, dt=F32):
            return st.enter_context(nc.sbuf_tensor("s_" + name, list(shape), dt))

        bank = [top.enter_context(nc.psum_tensor("p_bank%d" % i, [128, 512], F32)) for i in range(8)]
        b_bank = kb.bufs(8, "bank")
        for bb in b_bank:
            bb.excl = True

        idf = sb(top, "idf", [128, 128])
        idb = sb(top, "idb", [128, 128], BF16)
        pnwT = sb(top, "pnwT", [128, KC])
        utb = sb(top, "utb", [128, 128], BF16)
        b_const = kb.buf("const")
        kb.dma(idf[:], idf_d[:, :], writes=[b_const], chan="c0")
        kb.dma(pnwT[:], pnwT_d[:, :], writes=[b_const], chan="c1")
        kb.op("dve", lambda e: e.tensor_copy(out=idb[:], in_=idf[:]), reads=[b_const], writes=[b_const])

        b_hTd = kb.bufs(NB, "hTd")
        b_oaTd = [kb.bufs(NB, "oaTd%d_" % h) for h in range(NH)]
        b_obTd = [kb.bufs(NB, "obTd%d_" % h) for h in range(NH)]
        out_bufs = []

        def dump_bf16(name, src_fn, nblk, width, rbufs, oidx=None):
            with contextlib.ExitStack() as pd:
                tmp = sb(pd, "dbg_" + name, [128, width], BF16)
                tmpf = sb(pd, "dbgf_" + name, [128, width], F32)
                b_t = kb.buf("dbgt")
                for i in range(nblk):
                    kb.dma(tmp[:], src_fn(i), reads=[rbufs[i]], writes=[b_t], chan="dbg0")
                    kb.op("dve", lambda e: e.tensor_copy(out=tmpf[:], in_=tmp[:]), reads=[b_t], writes=[b_t])
                    bo = kb.buf("dbgo")
                    kb.dma(dbg_out[name][oidx[i] if oidx else i], tmpf[:], reads=[b_t], writes=[bo], chan="dbg1")
                    out_bufs.append(bo)

        with contextlib.ExitStack() as st_h:
            hT = sb(st_h, "hT", [128, KC, S], BF16)
            b_hT = kb.bufs(NB, "hT")

            with contextlib.ExitStack() as pa:
                xt = [sb(pa, "xt%d" % i, [128, D]) for i in range(2)]
                b_xt = kb.bufs(2, "xt")
                junk = sb(pa, "junkA", [128, D], BF16)
                b_junk = kb.buf("junkA")
                ssq = sb(pa, "ssqA", [128, 2])
                b_ssq = kb.bufs(2, "ssq")
                xn = [sb(pa, "xn%d" % i, [128, D], BF16) for i in range(2)]
                b_xn = kb.bufs(2, "xn")
                if dbg.get("skip_A"):
                    kb.op("dve", lambda e: e.memset(hT[:], 0.5), writes=b_hT)
                for t in range(0 if dbg.get("skip_A") else NT):
                    i = t % 2
                    blk = t // 4
                    pT = bank[i][:].bitcast(BF16).rearrange("p (k n) -> p k n", k=KC)
                    kb.dma(xt[i][:], x_d[t * 128:(t + 1) * 128, :], writes=[b_xt[i]], chan="ldx%d" % i)
                    kb.op("act", lambda e: e.activation(out=junk[:], in_=xt[i][:], func=AF.Square, accum_out=ssq[:, i:i + 1]),
                          reads=[b_xt[i]], writes=[b_junk, b_ssq[i]])
                    kb.op("act", lambda e: e.activation(out=ssq[:, i:i + 1], in_=ssq[:, i:i + 1], func=AF.Sqrt, bias=EPS, scale=1.0 / D),
                          reads=[b_ssq[i]], writes=[b_ssq[i]])
                    kb.op("dve", lambda e: e.reciprocal(out=ssq[:, i:i + 1], in_=ssq[:, i:i + 1]), reads=[b_ssq[i]], writes=[b_ssq[i]])
                    kb.op("dve", lambda e: e.tensor_scalar(out=xn[i][:], in0=xt[i][:], scalar1=ssq[:, i:i + 1], scalar2=None, op0=ALU.mult),
                          reads=[b_xt[i], b_ssq[i]], writes=[b_xn[i]])
                    for kc in range(KC):
                        kb.op("pe", lambda e, kc=kc: e.transpose(pT[:, kc, :], xn[i][:, kc * 128:(kc + 1) * 128], idb[:]),
                              reads=[b_xn[i], b_const], writes=[b_bank[i]], sig=(kc == KC - 1))
                    kb.op("dve", lambda e: e.tensor_tensor(out=hT[:, :, t * 128:(t + 1) * 128], in0=pT,
                                                          in1=pnwT[:].unsqueeze(2).to_broadcast([128, KC, 128]), op=ALU.mult),
                          reads=[b_bank[i], b_const], writes=[b_hT[blk]])
                    if t % 4 == 3:
                        kb.dma(hT_d[:, :, blk * 512:(blk + 1) * 512], hT[:, :, blk * 512:(blk + 1) * 512],
                               reads=[b_hT[blk]], writes=[b_hTd[blk]], chan="sth%d" % (blk % 2))
            if stop_after == "A":
                kb.finish(out_bufs + b_hTd)
                return nc

            if not dbg.get("no_barrier"):
                kb.barrier()
            if dbg.get("print_ops"):
                print("ops before phase B:", kb.nops)
            with contextlib.ExitStack() as pm:
                tabc = sb(pm, "tabc", [128, S])[0:32]
                tabs = sb(pm, "tabs", [128, S])[0:64]
                b_tab = kb.buf("tab")
                if not dbg.get("no_tabs"):
                    kb.dma(tabc[:], tabc_d[:, :], writes=[b_tab], chan="c0")
                    kb.dma(tabs[:], tabs_d[:, :], writes=[b_tab], chan="c0")
                utf = sb(pm, "utf", [128, 128])
                kb.dma(utf[:], ut_d[:, :], writes=[b_const], chan="c1")
                kb.op("dve", lambda e: e.tensor_copy(out=utb[:], in_=utf[:]), reads=[b_const], writes=[b_const])
                wst = sb(pm, "wst", [128, KC, WM_COLS])
                b_wst = kb.buf("wst")
                wb = [sb(pm, "wb%d" % i, [128, KC, WM_COLS], BF16) for i in range(2)]
                b_wb = kb.bufs(2, "wb")
                QT = sb(pm, "QT", [128, S], BF16)
                KT = sb(pm, "KT", [128, S], BF16)
                VA = sb(pm, "VA", [128, NT, 129], BF16)
                SZT = sb(pm, "SZT", [128, S], BF16)
                b_QT, b_KT, b_VA, b_SZT = kb.bufs(NB, "QT"), kb.bufs(NB, "KT"), kb.bufs(NB, "VA"), kb.bufs(NB, "SZT")
                b_KTall = kb.buf("KTall")
                rt1 = sb(pm, "rt1", [128, 512])[0:32]
                rt2 = sb(pm, "rt2", [128, 512])[0:32]
                rt3 = sb(pm, "rt3", [128, 512])[0:64]
                rt4 = sb(pm, "rt4", [128, 512])[0:32]
                b_rt1, b_rt2, b_rt3, b_rt4 = kb.buf("rt1"), kb.buf("rt2"), kb.buf("rt3"), kb.buf("rt4")
                km = sb(pm, "km", [128, 16])
                kmh = sb(pm, "kmh", [128, 16], BF16)
                kml = sb(pm, "kml", [128, 16], BF16)
                b_km = kb.buf("km")
                gs = sb(pm, "gs", [128, 4, 16])
                m8 = sb(pm, "m8", [128, 4, 8])
                sel = sb(pm, "sel", [128, 4, 16])
                b_gs, b_sel = kb.buf("gs"), kb.buf("sel")
                PT = [sb(pm, "PT%d" % i, [128, 512], BF16) for i in range(4)]
                b_PT = kb.bufs(4, "PT")
                oacc = sb(pm, "oacc", [128, 4, 129])
                b_oacc = kb.bufs(4, "oacc")
                rc = sb(pm, "rc", [128, 4])
                onb = sb(pm, "onb", [128, 4, 128], BF16)
                b_onb = kb.bufs(4, "onb")
                oast = [sb(pm, "oast%d" % i, [128, 512], BF16) for i in range(2)]
                b_oast = kb.bufs(2, "oast")
                if not dbg.get("no_memsets"):
                    kb.op("pool", lambda e: e.memset(VA[:, :, 128:129], 1.0), writes=b_VA)

                def load_w(h):
                    kb.dma(wst[:], wm_d[h], writes=[b_wst], chan="ldw")
                    kb.op("pool", lambda e: e.tensor_copy(out=wb[h % 2][:], in_=wst[:]), reads=[b_wst], writes=[b_wb[h % 2]])

                load_w(0)
                pjn = [0]

                def pj_next():
                    i = pjn[0] % 2
                    pjn[0] += 1
                    return i

                SC = float(HD) ** -0.5
                for h in range(dbg.get("moba_heads", NH)):
                    w = wb[h % 2]
                    bw = b_wb[h % 2]
                    if h + 1 < NH and not dbg.get("no_prefetch"):
                        load_w(h + 1)
                    if not dbg.get("no_memsets"):
                        kb.op("pool", lambda e: e.memset(gs[:], -1e30), writes=[b_gs])
                    for tb in range(dbg.get("proj_tb", NB)):
                        ts_ = slice(tb * 512, (tb + 1) * 512)
                        i = pj_next()
                        mm_group(kb, bank[i][:], [(w[:, kc, 0:128], hT[:, kc, ts_]) for kc in range(KC)], [bw, b_hT[tb]], [b_bank[i]])
                        if dbg.get("v478") == "dve":
                            kb.op("dve", lambda e: e.tensor_copy(out=QT[:, ts_], in_=bank[i][:]), reads=[b_bank[i]], writes=[b_QT[tb]])
                        elif dbg.get("v478") == "nopsum":
                            kb.op("act", lambda e: e.copy(out=QT[:, ts_], in_=hT[:, 0, ts_]), reads=[b_bank[i]], writes=[b_QT[tb]])
                        elif dbg.get("v478") == "bank3":
                            kb.op("act", lambda e: e.copy(out=QT[:, ts_], in_=bank[3][:]), reads=[b_bank[i]], writes=[b_QT[tb]])
                        elif dbg.get("v478") == "f32out":
                            kb.op("act", lambda e: e.copy(out=tabc[:, 0:512], in_=bank[i][0:32, :]), reads=[b_bank[i]], writes=[b_QT[tb]])
                        elif dbg.get("v478") == "szt":
                            kb.op("act", lambda e: e.copy(out=SZT[:, ts_], in_=bank[i][:]), reads=[b_bank[i]], writes=[b_QT[tb]])
                        else:
                            kb.op("act", lambda e: e.copy(out=QT[:, ts_], in_=bank[i][:]), reads=[b_bank[i]], writes=[b_QT[tb]])
                        if dbg.get("proj_parts") == "q":
                            continue
                        if not dbg.get("no_rope_dve"):
                            kb.op("dve", lambda e: e.tensor_tensor(out=rt1[:], in0=bank[i][0:32, :], in1=tabc[:, ts_], op=ALU.mult),
                                  reads=[b_bank[i], b_tab] + ([b_QT[tb]] if dbg.get("ser") else []), writes=[b_rt1])
                        i = pj_next()
                        mm_group(kb, bank[i][:], [(w[:, kc, 128:256], hT[:, kc, ts_]) for kc in range(KC)], [bw, b_hT[tb]], [b_bank[i]])
                        kb.op("act", lambda e: e.copy(out=KT[:, ts_], in_=bank[i][:]), reads=[b_bank[i]], writes=[b_KT[tb], b_KTall])
                        if not dbg.get("no_rope_dve"):
                            kb.op("dve", lambda e: e.tensor_tensor(out=rt2[:], in0=bank[i][0:32, :], in1=tabc[:, ts_], op=ALU.mult),
                                  reads=[b_bank[i], b_tab] + ([b_KT[tb]] if dbg.get("ser") else []), writes=[b_rt2])
                        if dbg.get("proj_parts") == "qk":
                            continue
                        i = pj_next()
                        mm_group(kb, bank[i][0:64, :], [(w[:, kc, 512:576], hT[:, kc, ts_]) for kc in range(KC)], [bw, b_hT[tb]], [b_bank[i]])
                        kb.op("dve", lambda e: e.tensor_tensor(out=rt3[:], in0=bank[i][0:64, :], in1=tabs[:, ts_], op=ALU.mult),
                              reads=[b_bank[i], b_tab], writes=[b_rt3])
                        kb.op("pool", lambda e: e.tensor_copy(out=rt4[:], in_=rt3[32:64, :]), reads=[b_rt3], writes=[b_rt4])
                        kb.op("pool", lambda e: e.tensor_tensor(out=QT[0:32, ts_], in0=rt1[:], in1=rt3[0:32, :], op=ALU.add),
                              reads=[b_rt1, b_rt3], writes=[b_QT[tb]])
                        kb.op("pool", lambda e: e.tensor_tensor(out=KT[0:32, ts_], in0=rt2[:], in1=rt4[:], op=ALU.add),
                              reads=[b_rt2, b_rt4], writes=[b_KT[tb], b_KTall])
                        if dbg.get("proj_parts") == "qks":
                            continue
                        i = pj_next()
                        for sub in range(4):
                            tk = slice(tb * 512 + sub * 128, tb * 512 + (sub + 1) * 128)
                            mm_group(kb, bank[i][:, sub * 128:(sub + 1) * 128], [(hT[:, kc, tk], w[:, kc, 256:384]) for kc in range(KC)],
                                     [bw, b_hT[tb]], [b_bank[i]])
                        kb.op("act", lambda e: e.copy(out=VA[:, tb * 4:(tb + 1) * 4, 0:128], in_=bank[i][:].rearrange("p (a b) -> p a b", a=4)),
                              reads=[b_bank[i]], writes=[b_VA[tb]])
                        if dbg.get("proj_parts") == "qksv":
                            continue
                        i = pj_next()
                        mm_group(kb, bank[i][:], [(w[:, kc, 384:512], hT[:, kc, ts_]) for kc in range(KC)], [bw, b_hT[tb]], [b_bank[i]])
                        kb.op("act", lambda e: e.activation(out=SZT[:, ts_], in_=bank[i][:], func=AF.Silu), reads=[b_bank[i]], writes=[b_SZT[tb]])
                    if dbg.get("print_ops"):
                        print("ops after proj head", h, kb.nops)
                    if dbg.get("proj_only"):
                        continue
                    kb.op("dve", lambda e: e.tensor_reduce(out=km[:], in_=KT[:].rearrange("p (n l) -> p n l", l=256), axis=AX.X, op=ALU.add),
                          reads=[b_KTall], writes=[b_km])
                    kb.op("act", lambda e: e.activation(out=kmh[:], in_=km[:], func=AF.Copy, scale=1.0 / 256), reads=[b_km], writes=[b_km])
                    kb.op("dve", lambda e: e.scalar_tensor_tensor(out=kml[:], in0=km[:], scalar=1.0 / 256, in1=kmh[:], op0=ALU.mult, op1=ALU.subtract),
                          reads=[b_km], writes=[b_km])
                    pacc_set = 0
                    psn = 0
                    ptn = 0
                    for st in range(dbg.get("moba_st", NB)):
                        b0, b1 = 2 * st, 2 * st + 1
                        qblk = [b0, b0, b1, b1]
                        qs = slice(st * 512, (st + 1) * 512)
                        use_sel = b1 >= 4
                        if use_sel:
                            pG = bank[7][:, 0:64].rearrange("p (a b) -> p a b", a=4)
                            for j in range(4):
                                qsl = slice(st * 512 + j * 128, st * 512 + (j + 1) * 128)
                                for ii, kmx in enumerate((kmh, kml)):
                                    kb.op("pe", lambda e, kmx=kmx, ii=ii: e.matmul(pG[:, j, :], lhsT=QT[:, qsl], rhs=kmx[:], start=(ii == 0), stop=(ii == 1)),
                                          reads=[b_QT[st], b_km], writes=[b_bank[7]], sig=(j == 3 and ii == 1))
                            for (ja, jb, bb) in ((0, 2, b0), (2, 4, b1)):
                                if bb >= 1:
                                    kb.op("dve", lambda e: e.tensor_copy(out=gs[:, ja:jb, 0:bb], in_=pG[:, ja:jb, 0:bb]), reads=[b_bank[7]], writes=[b_gs])
                            for j in range(4):
                                kb.op("dve", lambda e: e.max(out=m8[:, j, :], in_=gs[:, j, :]), reads=[b_gs], writes=[b_sel])
                            kb.op("dve", lambda e: e.tensor_tensor(out=sel[:], in0=gs[:], in1=m8[:, :, 2:3].to_broadcast([128, 4, 16]), op=ALU.is_ge),
                                  reads=[b_gs, b_sel], writes=[b_sel])
                        first = [True] * 4
                        for n in range(b1 + 1):
                            jlo = 0 if n <= b0 else 2
                            seta = (4, 5) if pacc_set == 0 else (0, 1)
                            pacc_set ^= 1
                            pacc = {}
                            for j in range(4):
                                bk = seta[j // 2]
                                pacc[j] = (bank[bk][:, (j % 2) * 129:(j % 2 + 1) * 129], b_bank[bk])
                            pts = {}
                            for c in range(2):
                                kt = 2 * n + c
                                jv = max(jlo, kt - 4 * st)
                                if jv > 3:
                                    continue
                                cols = slice(jv * 128, 512)
                                bS = 2 + (psn % 2)
                                psn += 1
                                kb.op("pe", lambda e: e.matmul(bank[bS][:, cols], lhsT=KT[:, kt * 128:(kt + 1) * 128], rhs=QT[:, st * 512 + jv * 128:(st + 1) * 512],
                                                               start=True, stop=True),
                                      reads=[b_KT[kt // 4], b_QT[st]], writes=[b_bank[bS]])
                                p = ptn % 4
                                ptn += 1
                                pts[c] = (p, jv, kt)
                                kb.op("act", lambda e: e.activation(out=PT[p][:, cols], in_=bank[bS][:, cols], func=AF.Exp, scale=SC),
                                      reads=[b_bank[bS]], writes=[b_PT[p]])
                                if kt >= 4 * st and kt - 4 * st == jv:
                                    dsl = slice(jv * 128, (jv + 1) * 128)
                                    kb.op("pool", lambda e: e.tensor_tensor(out=PT[p][:, dsl], in0=PT[p][:, dsl], in1=utb[:], op=ALU.mult),
                                          reads=[b_PT[p], b_const], writes=[b_PT[p]])
                            for j in range(jlo, 4):
                                ap, bb_ = pacc[j]
                                vis = [c for c in pts if pts[c][1] <= j]
                                for ci, c in enumerate(vis):
                                    p, jv, kt = pts[c]
                                    kb.op("pe", lambda e: e.matmul(ap, lhsT=PT[p][:, j * 128:(j + 1) * 128], rhs=VA[:, kt, :],
                                                                   start=(ci == 0), stop=(ci == len(vis) - 1)),
                                          reads=[b_PT[p], b_VA[kt // 4]], writes=[bb_], sig=(ci == len(vis) - 1))
                            for j in range(jlo, 4):
                                ap, bb_ = pacc[j]
                                own = (n == qblk[j])
                                if own or not use_sel or qblk[j] < 4:
                                    if first[j]:
                                        kb.op("dve", lambda e: e.tensor_copy(out=oacc[:, j, :], in_=ap), reads=[bb_], writes=[b_oacc[j]])
                                    else:
                                        kb.op("dve", lambda e: e.tensor_tensor(out=oacc[:, j, :], in0=ap, in1=oacc[:, j, :], op=ALU.add),
                                              reads=[bb_, b_oacc[j]], writes=[b_oacc[j]])
                                else:
                                    if first[j]:
                                        kb.op("dve", lambda e: e.tensor_scalar(out=oacc[:, j, :], in0=ap, scalar1=sel[:, j, n:n + 1], scalar2=None, op0=ALU.mult),
                                              reads=[bb_, b_sel], writes=[b_oacc[j]])
                                    else:
                                        kb.op("dve", lambda e: e.scalar_tensor_tensor(out=oacc[:, j, :], in0=ap, scalar=sel[:, j, n:n + 1], in1=oacc[:, j, :],
                                                                                     op0=ALU.mult, op1=ALU.add),
                                              reads=[bb_, b_sel, b_oacc[j]], writes=[b_oacc[j]])
                                first[j] = False
                        pTv = bank[6][:].bitcast(BF16)[:, 0:512].rearrange("p (a b) -> p a b", a=4)
                        for j in range(4):
                            kb.op("dve", lambda e: e.reciprocal(out=rc[:, j:j + 1], in_=oacc[:, j, 128:129]), reads=[b_oacc[j]], writes=[b_onb[j]])
                            kb.op("dve", lambda e: e.tensor_scalar(out=onb[:, j, :], in0=oacc[:, j, 0:128], scalar1=rc[:, j:j + 1], scalar2=None, op0=ALU.mult),
                                  reads=[b_oacc[j], b_onb[j]], writes=[b_onb[j]])
                            kb.op("pe", lambda e: e.transpose(pTv[:, j, :], onb[:, j, :], idb[:]), reads=[b_onb[j], b_const], writes=[b_bank[6]], sig=(j == 3))
                        o = st % 2
                        kb.op("dve", lambda e: e.tensor_tensor(out=oast[o][:], in0=bank[6][:].bitcast(BF16)[:, 0:512], in1=SZT[:, qs], op=ALU.mult),
                              reads=[b_bank[6], b_SZT[st]], writes=[b_oast[o]])
                        kb.dma(oaT_d[h][:, qs], oast[o][:], reads=[b_oast[o]], writes=[b_oaTd[h][st]], chan="sto%d" % o)
            if "oaT" in dbg_out:
                idx = [h * NB + st for h in range(dbg.get("moba_heads", NH)) for st in range(dbg.get("moba_st", NB))]
                dump_bf16("oaT", lambda i: oaT_d[idx[i] // NB][:, (idx[i] % NB) * 512:(idx[i] % NB + 1) * 512], len(idx), 512,
                          [b_oaTd[i // NB][i % NB] for i in idx], idx)
            if stop_after == "B":
                kb.barrier(("sp",))
                assert kb.check_deadlock()
                return nc

            kb.barrier()
            with contextlib.ExitStack() as pg:
                utf2 = sb(pg, "utf2", [128, 128])
                negt = sb(pg, "negt", [128, 128])
                smask = sb(pg, "smask", [128, 128])
                ones = sb(pg, "ones", [128, 128])
                cw = sb(pg, "cw", [128, 24, 4])
                dtb = sb(pg, "dtb", [128, NH])
                nega = sb(pg, "nega", [128, NH])
                gnw = sb(pg, "gnw", [128, 1])
                b_gc = kb.buf("gconst")
                for t_, d_ in ((utf2, ut_d), (negt, negt_d), (smask, smask_d), (cw, cw_d), (dtb, dtb_d), (nega, alog_d), (gnw, gnw_d)):
                    kb.dma(t_[:], d_, writes=[b_gc], chan="c0")
                kb.op("pool", lambda e: e.memset(ones[:], 1.0), writes=[b_gc])
                kb.op("act", lambda e: e.activation(out=nega[:], in_=nega[:], func=AF.Exp), reads=[b_gc], writes=[b_gc])
                kb.op("dve", lambda e: e.tensor_scalar(out=nega[:], in0=nega[:], scalar1=-1.0, scalar2=None, op0=ALU.mult), reads=[b_gc], writes=[b_gc])
                wgs = sb(pg, "wgs", [128, KC, 514])
                b_wgs = kb.buf("wgs")
                wgb = [sb(pg, "wgb0", [128, KC, 514], BF16)] * 2
                b_wgb = [kb.buf("wgb")] * 2
                RAW = sb(pg, "RAW", [128, S + 3])
                b_RAW = kb.buf("RAW")
                qkv = [sb(pg, "gq", [128, S]), sb(pg, "gk", [128, S]), sb(pg, "gv", [128, S])]
                b_qkv = kb.bufs(3, "gqkv")
                szb = sb(pg, "szb", [128, S], BF16)
                b_szb = kb.buf("szb")
                tsq = sb(pg, "tsq", [128, 512])
                rn = sb(pg, "rn", [128, 512])
                b_tsq, b_rn = kb.buf("tsq"), kb.buf("rn")
                NU = S // 128
                sm = {n: sb(pg, "g_" + n, [128, NU]) for n in ("beta", "negb", "sp", "gall", "Gtok", "negG", "eG", "kdsc", "glast")}
                b_sm = kb.buf("gsmall")

                def t128(name, w=128, dt=F32):
                    return sb(pg, "u_" + name, [128, w], dt), kb.buf("u_" + name)

                NSLOT = 4
                slots = []
                for s_ in range(NSLOT):
                    slots.append({nm: t128("%s_%d" % (nm, s_), w_) for nm, w_ in (("diagG", 128), ("decT", 128), ("decTs", 128), ("QKdT", 128), ("PQ0", 256),
                                                                                  ("PQ1", 256), ("X0", 128), ("X1", 128), ("keG", 128), ("kdec", 128),
                                                                                  ("vtok", 128), ("bU", 128), ("WT", 128))})
                vnew, b_vnew = t128("vnew")
                SS = [t128("S%d" % i) for i in range(2)]
                t1, b_t1 = t128("t1")
                oo, b_oo = t128("oo")
                on_, b_on = t128("on")
                junkg, b_junkg = t128("junkg")
                ms, b_ms = t128("ms", 2)
                obst = [sb(pg, "obst%d" % i, [128, 512], BF16) for i in range(2)]
                b_obst = kb.bufs(2, "obst")
                kb.op("pool", lambda e: e.memset(RAW[:, 0:3], 0.0), writes=[b_RAW])

                def load_wg(h):
                    kb.dma(wgs[:], wg_d[h], writes=[b_wgs], chan="ldw")

                def cast_wg(h):
                    kb.op("pool", lambda e: e.tensor_copy(out=wgb[0][:], in_=wgs[:]), reads=[b_wgs], writes=[b_wgb[0]])

                n_gh = dbg.get("gdn_heads", NH)
                n_gu = dbg.get("gdn_units", NU)
                load_wg(0)
                for h in range(n_gh):
                    w = wgb[0]
                    bw = b_wgb[0]
                    cast_wg(h)
                    if h + 1 < n_gh:
                        load_wg(h + 1)
                    def proj_conv(c):
                        dst, bd = qkv[c], b_qkv[c]
                        for tb in range(NB):
                            ts_ = slice(tb * 512, (tb + 1) * 512)
                            i = pj_next()
                            mm_group(kb, bank[i][:], [(w[:, kc, c * 128:(c + 1) * 128], hT[:, kc, ts_]) for kc in range(KC)], [bw, b_hT[tb]], [b_bank[i]])
                            kb.op("act", lambda e: e.copy(out=RAW[:, 3 + tb * 512:3 + (tb + 1) * 512], in_=bank[i][:]), reads=[b_bank[i]], writes=[b_RAW])
                        ct = c * NH + h
                        kb.op("dve", lambda e: e.tensor_scalar(out=dst[:], in0=RAW[:, 3:S + 3], scalar1=cw[:, ct, 3:4], scalar2=None, op0=ALU.mult),
                              reads=[b_RAW, b_gc], writes=[bd])
                        for j in (2, 1, 0):
                            kb.op("dve", lambda e, j=j: e.scalar_tensor_tensor(out=dst[:], in0=RAW[:, j:S + j], scalar=cw[:, ct, j:j + 1], in1=dst[:],
                                                                            op0=ALU.mult, op1=ALU.add),
                                  reads=[b_RAW, b_gc, bd], writes=[bd])
                        kb.op("act", lambda e: e.activation(out=dst[:], in_=dst[:], func=AF.Silu), reads=[bd], writes=[bd])

                    def l2n(c):
                        dst, bd = qkv[c], b_qkv[c]
                        for tb in range(NB):
                            ts_ = slice(tb * 512, (tb + 1) * 512)
                            kb.op("pool", lambda e: e.tensor_tensor(out=tsq[:], in0=dst[:, ts_], in1=dst[:, ts_], op=ALU.mult), reads=[bd], writes=[b_tsq])
                            i = pj_next()
                            kb.op("pe", lambda e: e.matmul(bank[i][:], lhsT=ones[:], rhs=tsq[:], start=True, stop=True), reads=[b_tsq, b_gc], writes=[b_bank[i]])
                            scl = float(HD) if c == 0 else 1.0
                            kb.op("act", lambda e: e.activation(out=rn[:], in_=bank[i][:], func=AF.Sqrt, bias=EPS * scl, scale=scl), reads=[b_bank[i]], writes=[b_rn])
                            kb.op("dve", lambda e: e.reciprocal(out=rn[:], in_=rn[:]), reads=[b_rn], writes=[b_rn])
                            kb.op("dve", lambda e: e.tensor_tensor(out=dst[:, ts_], in0=dst[:, ts_], in1=rn[:], op=ALU.mult), reads=[bd, b_rn], writes=[bd])

                    proj_conv(0)
                    proj_conv(1)
                    l2n(0)
                    proj_conv(2)
                    l2n(1)
                    for tb in range(NB):
                        ts_ = slice(tb * 512, (tb + 1) * 512)
                        i = pj_next()
                        mm_group(kb, bank[i][:], [(w[:, kc, 384:512], hT[:, kc, ts_]) for kc in range(KC)], [bw, b_hT[tb]], [b_bank[i]])
                        kb.op("act", lambda e: e.activation(out=szb[:, ts_], in_=bank[i][:], func=AF.Silu), reads=[b_bank[i]], writes=[b_szb])
                    lg = bank[2][:, 0:2 * NU].rearrange("p (u c) -> p u c", c=2)
                    for u in range(NU):
                        for kc in range(KC):
                            kb.op("pe", lambda e, kc=kc: e.matmul(lg[:, u, :], lhsT=hT[:, kc, u * 128:(u + 1) * 128], rhs=w[:, kc, 512:514],
                                                                 start=(kc == 0), stop=(kc == KC - 1)),
                                  reads=[bw, b_hT[u // 4]], writes=[b_bank[2]], sig=(u == NU - 1 and kc == KC - 1))
                    kb.op("act", lambda e: e.activation(out=sm["beta"][:], in_=lg[:, :, 0], func=AF.Sigmoid), reads=[b_bank[2]], writes=[b_sm])
                    kb.op("act", lambda e: e.activation(out=sm["sp"][:], in_=lg[:, :, 1], func=AF.Exp, bias=dtb[:, h:h + 1]), reads=[b_bank[2], b_gc], writes=[b_sm])
                    kb.op("act", lambda e: e.activation(out=sm["sp"][:], in_=sm["sp"][:], func=AF.Ln, bias=1.0), reads=[b_sm], writes=[b_sm])
                    kb.op("dve", lambda e: e.tensor_scalar(out=sm["gall"][:], in0=sm["sp"][:], scalar1=nega[:, h:h + 1], scalar2=None, op0=ALU.mult),
                          reads=[b_sm, b_gc], writes=[b_sm])
                    kb.op("dve", lambda e: e.tensor_scalar(out=sm["negb"][:], in0=sm["beta"][:], scalar1=-1.0, scalar2=None, op0=ALU.mult), reads=[b_sm], writes=[b_sm])
                    kb.op("pe", lambda e: e.matmul(bank[2][:, 64:64 + NU], lhsT=utf2[:], rhs=sm["gall"][:], start=True, stop=True), reads=[b_sm, b_gc], writes=[b_bank[2]], sig=False)
                    kb.op("pe", lambda e: e.matmul(bank[2][:, 128:128 + NU], lhsT=ones[:], rhs=sm["gall"][:], start=True, stop=True), reads=[b_sm, b_gc], writes=[b_bank[2]])
                    kb.op("dve", lambda e: e.tensor_copy(out=sm["Gtok"][:], in_=bank[2][:, 64:64 + NU]), reads=[b_bank[2]], writes=[b_sm])
                    kb.op("dve", lambda e: e.tensor_scalar(out=sm["negG"][:], in0=sm["Gtok"][:], scalar1=-1.0, scalar2=None, op0=ALU.mult), reads=[b_sm], writes=[b_sm])
                    kb.op("act", lambda e: e.activation(out=sm["glast"][:], in_=bank[2][:, 128:128 + NU], func=AF.Exp), reads=[b_bank[2]], writes=[b_sm])
                    kb.op("dve", lambda e: e.tensor_tensor(out=sm["kdsc"][:], in0=bank[2][:, 128:128 + NU], in1=sm["Gtok"][:], op=ALU.subtract), reads=[b_bank[2], b_sm], writes=[b_sm])
                    kb.op("act", lambda e: e.activation(out=sm["kdsc"][:], in_=sm["kdsc"][:], func=AF.Exp), reads=[b_sm], writes=[b_sm])
                    kb.op("act", lambda e: e.activation(out=sm["eG"][:], in_=sm["Gtok"][:], func=AF.Exp), reads=[b_sm], writes=[b_sm])
                    kb.op("pool", lambda e: e.memset(SS[0][0][:], 0.0), writes=[SS[0][1]])
                    qT_, kT_, vT_ = qkv

                    def solve(u):
                        sl = slots[u % NSLOT]
                        bA, bB = 2 + (u % NSLOT), 6 + (u % NSLOT) // 2
                        yo = 256 * ((u % NSLOT) % 2)
                        us = slice(u * 128, (u + 1) * 128)
                        col = lambda n: sm[n][:, u:u + 1]
                        diagG, b_diagG = sl["diagG"]
                        decT, b_decT = sl["decT"]
                        decTs, b_decTs = sl["decTs"]
                        QKdT, b_QKdT = sl["QKdT"]
                        PQ = [sl["PQ0"], sl["PQ1"]]
                        XX = [sl["X0"], sl["X1"]]
                        keG, b_keG = sl["keG"]
                        kdec, b_kdec = sl["kdec"]
                        vtok, b_vtok = sl["vtok"]
                        bU, b_bU = sl["bU"]
                        WT, b_WT = sl["WT"]
                        kb.op("pe", lambda e: e.matmul(bank[bB][:, yo:yo + 128], lhsT=kT_[:, us], rhs=kT_[:, us], start=True, stop=True), reads=[b_qkv[1]], writes=[b_bank[bB]], sig=False)
                        kb.op("pe", lambda e: e.matmul(bank[bB][:, yo + 128:yo + 256], lhsT=kT_[:, us], rhs=qT_[:, us], start=True, stop=True), reads=[b_qkv[1], b_qkv[0]], writes=[b_bank[bB]])
                        kb.op("dve", lambda e: e.tensor_scalar(out=diagG[:], in0=idf[:], scalar1=col("Gtok"), scalar2=None, op0=ALU.mult),
                              reads=[b_const, b_sm], writes=[b_diagG])
                        yield
                        kb.op("pe", lambda e: e.matmul(bank[bA][:, 0:128], lhsT=ones[:], rhs=diagG[:], start=True, stop=False), reads=[b_diagG, b_gc], writes=[b_bank[bA]], sig=False)
                        kb.op("pe", lambda e: e.matmul(bank[bA][:, 0:128], lhsT=idf[:], rhs=negt[:], start=False, stop=True), reads=[b_const, b_gc], writes=[b_bank[bA]])
                        yield
                        kb.op("act", lambda e: e.activation(out=decT[:], in_=bank[bA][:, 0:128], func=AF.Exp, bias=col("negG")), reads=[b_bank[bA], b_sm], writes=[b_decT])
                        yield
                        kb.op("pool", lambda e: e.tensor_tensor(out=decTs[:], in0=decT[:], in1=smask[:], op=ALU.mult), reads=[b_decT, b_gc], writes=[b_decTs])
                        kb.op("dve", lambda e: e.tensor_tensor(out=QKdT[:], in0=bank[bB][:, yo + 128:yo + 256], in1=decT[:], op=ALU.mult), reads=[b_bank[bB], b_decT], writes=[b_QKdT])
                        yield
                        Q0, bQ0 = PQ[0]
                        kb.op("dve", lambda e: e.scalar_tensor_tensor(out=Q0[:, 128:256], in0=bank[bB][:, yo:yo + 128], scalar=col("negb"), in1=decTs[:],
                                                                     op0=ALU.mult, op1=ALU.mult),
                              reads=[b_bank[bB], b_sm, b_decTs], writes=[bQ0])
                        yield
                        kb.op("pe", lambda e: e.transpose(bank[bA][:, 256:384], Q0[:, 128:256], idf[:]), reads=[bQ0, b_const], writes=[b_bank[bA]])
                        kb.op("pe", lambda e: e.transpose(bank[bB][:, yo:yo + 128], kT_[:, us], idf[:]), reads=[b_qkv[1], b_const], writes=[b_bank[bB]], sig=False)
                        kb.op("pe", lambda e: e.transpose(bank[bB][:, yo + 128:yo + 256], vT_[:, us], idf[:]), reads=[b_qkv[2], b_const], writes=[b_bank[bB]])
                        yield
                        kb.op("act", lambda e: e.copy(out=Q0[:, 0:128], in_=bank[bA][:, 256:384]), reads=[b_bank[bA]], writes=[bQ0])
                        X0, bX0 = XX[0]
                        kb.op("pool", lambda e: e.tensor_tensor(out=X0[:], in0=Q0[:, 128:256], in1=idf[:], op=ALU.add), reads=[bQ0, b_const], writes=[bX0])
                        kb.op("act", lambda e: e.activation(out=keG[:], in_=bank[bB][:, yo:yo + 128], func=AF.Copy, scale=col("eG")), reads=[b_bank[bB], b_sm], writes=[b_keG])
                        kb.op("dve", lambda e: e.tensor_scalar(out=kdec[:], in0=bank[bB][:, yo:yo + 128], scalar1=col("kdsc"), scalar2=None, op0=ALU.mult),
                              reads=[b_bank[bB], b_sm], writes=[b_kdec])
                        kb.op("act", lambda e: e.copy(out=vtok[:], in_=bank[bB][:, yo + 128:yo + 256]), reads=[b_bank[bB]], writes=[b_vtok])
                        yield
                        xi = 0
                        for lv in range(1, 7):
                            Pp, bPp = PQ[(lv - 1) % 2]
                            Pn, bPn = PQ[lv % 2]
                            kb.op("pe", lambda e: e.matmul(bank[bA][:, 256:384], lhsT=Pp[:, 128:256], rhs=Pp[:, 0:128], start=True, stop=True),
                                  reads=[bPp], writes=[b_bank[bA]], sig=(lv == 6))
                            if lv < 6:
                                kb.op("pe", lambda e: e.matmul(bank[bA][:, 384:512], lhsT=Pp[:, 0:128], rhs=Pp[:, 128:256], start=True, stop=True),
                                      reads=[bPp], writes=[b_bank[bA]])
                            yield
                            if lv < 6:
                                kb.op("act", lambda e: e.copy(out=Pn[:], in_=bank[bA][:, 256:512]), reads=[b_bank[bA]], writes=[bPn])
                            else:
                                kb.op("act", lambda e: e.copy(out=Pn[:, 0:128], in_=bank[bA][:, 256:384]), reads=[b_bank[bA]], writes=[bPn])
                            yield
                            Xc, bXc = XX[xi]
                            Xn, bXn = XX[1 - xi]
                            kb.op("pe", lambda e: e.matmul(bank[bA][:, 128:256], lhsT=Pn[:, 0:128], rhs=Xc[:], start=True, stop=True), reads=[bPn, bXc], writes=[b_bank[bA]])
                            yield
                            kb.op("dve", lambda e: e.tensor_tensor(out=Xn[:], in0=bank[bA][:, 128:256], in1=Xc[:], op=ALU.add), reads=[b_bank[bA], bXc], writes=[bXn])
                            xi = 1 - xi
                            yield
                        Xf, bXf = XX[xi]
                        kb.op("pe", lambda e: e.matmul(bank[bB][:, yo:yo + 128], lhsT=Xf[:], rhs=vtok[:], start=True, stop=True), reads=[bXf, b_vtok], writes=[b_bank[bB]], sig=False)
                        kb.op("pe", lambda e: e.matmul(bank[bB][:, yo + 128:yo + 256], lhsT=keG[:], rhs=Xf[:], start=True, stop=True), reads=[bXf, b_keG], writes=[b_bank[bB]])
                        yield
                        kb.op("dve", lambda e: e.tensor_scalar(out=bU[:], in0=bank[bB][:, yo:yo + 128], scalar1=col("beta"), scalar2=None, op0=ALU.mult),
                              reads=[b_bank[bB], b_sm], writes=[b_bU])
                        kb.op("act", lambda e: e.copy(out=WT[:], in_=bank[bB][:, yo + 128:yo + 256]), reads=[b_bank[bB]], writes=[b_WT])

                    def scan(u):
                        sl = slots[u % NSLOT]
                        us = slice(u * 128, (u + 1) * 128)
                        col = lambda n: sm[n][:, u:u + 1]
                        QKdT, b_QKdT = sl["QKdT"]
                        kdec, b_kdec = sl["kdec"]
                        bU, b_bU = sl["bU"]
                        WT, b_WT = sl["WT"]
                        Sc, bSc = SS[u % 2]
                        Sn, bSn = SS[(u + 1) % 2]
                        kb.op("pe", lambda e: e.matmul(bank[0][:, 0:128], lhsT=WT[:], rhs=Sc[:], start=True, stop=True), reads=[b_WT, bSc], writes=[b_bank[0]], sig=False)
                        kb.op("pe", lambda e: e.matmul(bank[0][:, 128:256], lhsT=qT_[:, us], rhs=Sc[:], start=True, stop=True), reads=[b_qkv[0], bSc], writes=[b_bank[0]])
                        yield
                        kb.op("dve", lambda e: e.scalar_tensor_tensor(out=vnew[:], in0=bank[0][:, 0:128], scalar=col("negb"), in1=bU[:], op0=ALU.mult, op1=ALU.add),
                              reads=[b_bank[0], b_sm, b_bU], writes=[b_vnew])
                        kb.op("act", lambda e: e.activation(out=t1[:], in_=bank[0][:, 128:256], func=AF.Copy, scale=col("eG")), reads=[b_bank[0], b_sm], writes=[b_t1])
                        yield
                        kb.op("pe", lambda e: e.matmul(bank[1][:, 128:256], lhsT=kdec[:], rhs=vnew[:], start=True, stop=True), reads=[b_kdec, b_vnew], writes=[b_bank[1]], sig=False)
                        kb.op("pe", lambda e: e.matmul(bank[1][:, 0:128], lhsT=QKdT[:], rhs=vnew[:], start=True, stop=True), reads=[b_QKdT, b_vnew], writes=[b_bank[1]])
                        yield
                        kb.op("dve", lambda e: e.scalar_tensor_tensor(out=Sn[:], in0=Sc[:], scalar=col("glast"), in1=bank[1][:, 128:256], op0=ALU.mult, op1=ALU.add),
                              reads=[bSc, b_sm, b_bank[1]], writes=[bSn])
                        kb.op("dve", lambda e: e.tensor_tensor(out=oo[:], in0=bank[1][:, 0:128], in1=t1[:], op=ALU.add), reads=[b_bank[1], b_t1], writes=[b_oo])
                        yield
                        kb.op("act", lambda e: e.activation(out=junkg[:], in_=oo[:], func=AF.Square, accum_out=ms[:, 0:1]), reads=[b_oo], writes=[b_junkg, b_ms])
                        kb.op("act", lambda e: e.activation(out=ms[:, 0:1], in_=ms[:, 0:1], func=AF.Sqrt, bias=EPS, scale=1.0 / HD), reads=[b_ms], writes=[b_ms])
                        yield
                        kb.op("dve", lambda e: e.reciprocal(out=ms[:, 0:1], in_=ms[:, 0:1]), reads=[b_ms], writes=[b_ms])
                        kb.op("dve", lambda e: e.tensor_scalar(out=on_[:], in0=oo[:], scalar1=ms[:, 0:1], scalar2=None, op0=ALU.mult), reads=[b_oo, b_ms], writes=[b_on])
                        yield
                        kb.op("pe", lambda e: e.transpose(bank[1][:, 256:384], on_[:], idf[:]), reads=[b_on, b_const], writes=[b_bank[1]])
                        yield
                        ob_i = (u // 4) % 2
                        kb.op("dve", lambda e: e.scalar_tensor_tensor(out=obst[ob_i][:, (u % 4) * 128:(u % 4 + 1) * 128], in0=bank[1][:, 256:384], scalar=gnw[:, 0:1],
                                                                     in1=szb[:, us], op0=ALU.mult, op1=ALU.mult),
                              reads=[b_bank[1], b_gc, b_szb], writes=[b_obst[ob_i]])
                        if u % 4 == 3:
                            blk = u // 4
                            kb.dma(obT_d[h][:, blk * 512:(blk + 1) * 512], obst[ob_i][:], reads=[b_obst[ob_i]], writes=[b_obTd[h][blk]], chan="sto%d" % ob_i)

                    active, done_solve = [], set()
                    su, cu, scan_g = 0, 0, None
                    while cu < n_gu:
                        while su < n_gu and su < cu + NSLOT and len(active) < NSLOT:
                            active.append((su, solve(su)))
                            su += 1
                        for item in list(active):
                            try:
                                next(item[1])
                            except StopIteration:
                                active.remove(item)
                                done_solve.add(item[0])
                        if scan_g is None and cu in done_solve:
                            scan_g = scan(cu)
                        if scan_g is not None:
                            try:
                                next(scan_g)
                            except StopIteration:
                                scan_g = None
                                cu += 1
            if "obT" in dbg_out:
                idx = [h * NB + st for h in range(dbg.get("gdn_heads", NH)) for st in range(dbg.get("gdn_units", S // 128) // 4)]
                dump_bf16("obT", lambda i: obT_d[idx[i] // NB][:, (idx[i] % NB) * 512:(idx[i] % NB + 1) * 512], len(idx), 512,
                          [b_obTd[i // NB][i % NB] for i in idx], idx)
            if stop_after == "C":
                kb.barrier(("sp",))
                assert kb.check_deadlock()
                return nc


        kb.barrier()
        with contextlib.ExitStack() as pe_:
            wstE = sb(pe_, "wstE", [128, KC, 512])
            b_wstE = kb.buf("wstE")
            wE = {n: sb(pe_, "wE_" + n, [128, KC, D], BF16) for n in ("wa", "wb", "wo", "wga", "wgb")}
            b_wE = kb.buf("wE")
            for n, d_ in (("wa", wa_d), ("wb", wb_d), ("wo", wo_d), ("wga", wga_d), ("wgb", wgb_d)):
                for hf in range(2):
                    kb.dma(wstE[:], d_[:, :, hf * 512:(hf + 1) * 512], writes=[b_wstE], chan="ldw")
                    kb.op("pool", lambda e: e.tensor_copy(out=wE[n][:, :, hf * 512:(hf + 1) * 512], in_=wstE[:]), reads=[b_wstE], writes=[b_wE])
            pnw = sb(pe_, "pnw", [128, D])
            kb.dma(pnw[:], pnw_d, writes=[b_wE], chan="c0")
            hTb = sb(pe_, "hTb", [128, KC, 512], BF16)
            oab = sb(pe_, "oab", [128, NH, 512], BF16)
            obb = sb(pe_, "obb", [128, NH, 512], BF16)
            b_hTb, b_oab, b_obb = kb.buf("hTb"), kb.buf("oab"), kb.buf("obb")
            mT = sb(pe_, "mT", [128, KC, 512], BF16)
            b_mT = kb.buf("mT")
            sg = [sb(pe_, "sg%d" % i, [128, 512]) for i in range(4)]
            m12 = [sb(pe_, "m12_%d" % i, [128, 512]) for i in range(4)]
            b_sg, b_m12 = kb.bufs(4, "sg"), kb.bufs(4, "m12")
            xs_ = [sb(pe_, "xs%d" % i, [128, D]) for i in range(2)]
            b_xs = kb.bufs(2, "xs")
            yt = [sb(pe_, "yt%d" % i, [128, D]) for i in range(2)]
            b_yt = kb.bufs(2, "yt")
            junkE = sb(pe_, "junkE", [128, 512])
            b_junkE = kb.buf("junkE")
            ssE = sb(pe_, "ssE", [128, 4])
            b_ssE = kb.buf("ssE")
            nsub = 0
            for tb in range(NB):
                ts_ = slice(tb * 512, (tb + 1) * 512)
                kb.dma(hTb[:], hT_d[:, :, ts_], reads=[b_hTd[tb]], writes=[b_hTb], chan="ldx0")
                kb.dma(oab[:], oaT_d[:, :, ts_].rearrange("h p t -> p h t"), reads=[b_oaTd[h][tb] for h in range(NH)], writes=[b_oab], chan="ldx1")
                kb.dma(obb[:], obT_d[:, :, ts_].rearrange("h p t -> p h t"), reads=[b_obTd[h][tb] for h in range(NH)], writes=[b_obb], chan="ldx0")
                for dc in range(KC):
                    dsl = slice(dc * 128, (dc + 1) * 128)
                    par = dc % 2
                    for br, (wy, wgt, src, bsrc) in enumerate((("wa", "wga", oab, b_oab), ("wb", "wgb", obb, b_obb))):
                        by, bg = 2 * br + 4 * par, 2 * br + 1 + 4 * par
                        si = 2 * par + br
                        mm_group(kb, bank[by][:], [(wE[wy][:, e_, dsl], src[:, e_, :]) for e_ in range(NH)], [b_wE, bsrc], [b_bank[by]])
                        mm_group(kb, bank[bg][:], [(wE[wgt][:, kc, dsl], hTb[:, kc, :]) for kc in range(KC)], [b_wE, b_hTb], [b_bank[bg]])
                        kb.op("act", lambda e: e.activation(out=sg[si][:], in_=bank[bg][:], func=AF.Sigmoid), reads=[b_bank[bg]], writes=[b_sg[si]])
                        kb.op("dve", lambda e: e.tensor_tensor(out=m12[si][:], in0=bank[by][:], in1=sg[si][:], op=ALU.mult), reads=[b_bank[by], b_sg[si]], writes=[b_m12[si]])
                    kb.op("pool", lambda e: e.tensor_tensor(out=mT[:, dc, :], in0=m12[2 * par][:], in1=m12[2 * par + 1][:], op=ALU.add),
                          reads=[b_m12[2 * par], b_m12[2 * par + 1]], writes=[b_mT])
                for sub in range(4):
                    i = nsub % 2
                    nsub += 1
                    r0 = tb * 512 + sub * 128
                    kb.dma(xs_[i][:], x_d[r0:r0 + 128, :], writes=[b_xs[i]], chan="ldx%d" % i)
                    for hf in range(2):
                        bo = 4 + hf
                        mm_group(kb, bank[bo][:], [(mT[:, dc, sub * 128:(sub + 1) * 128], wE["wo"][:, dc, hf * 512:(hf + 1) * 512]) for dc in range(KC)],
                                 [b_mT, b_wE], [b_bank[bo]])
                        kb.op("act", lambda e: e.activation(out=junkE[:], in_=bank[bo][:], func=AF.Square, accum_out=ssE[:, hf:hf + 1]),
                              reads=[b_bank[bo]], writes=[b_junkE, b_ssE])
                    kb.op("dve", lambda e: e.tensor_tensor(out=ssE[:, 2:3], in0=ssE[:, 0:1], in1=ssE[:, 1:2], op=ALU.add), reads=[b_ssE], writes=[b_ssE])
                    kb.op("act", lambda e: e.activation(out=ssE[:, 2:3], in_=ssE[:, 2:3], func=AF.Sqrt, bias=EPS, scale=1.0 / D), reads=[b_ssE], writes=[b_ssE])
                    kb.op("dve", lambda e: e.reciprocal(out=ssE[:, 3:4], in_=ssE[:, 2:3]), reads=[b_ssE], writes=[b_ssE])
                    for hf in range(2):
                        bo = 4 + hf
                        hs = slice(hf * 512, (hf + 1) * 512)
                        kb.op("dve", lambda e: e.scalar_tensor_tensor(out=yt[i][:, hs], in0=bank[bo][:], scalar=ssE[:, 3:4], in1=pnw[:, hs], op0=ALU.mult, op1=ALU.mult),
                              reads=[b_bank[bo], b_ssE, b_wE], writes=[b_yt[i]])
                    kb.op("pool", lambda e: e.tensor_tensor(out=yt[i][:], in0=yt[i][:], in1=xs_[i][:], op=ALU.add), reads=[b_yt[i], b_xs[i]], writes=[b_yt[i]])
                    bo_ = kb.buf("yout")
                    kb.dma(y_d[r0:r0 + 128, :], yt[i][:], reads=[b_yt[i]], writes=[bo_], chan="sty%d" % i)
                    out_bufs.append(bo_)
        kb.finish(out_bufs)
        kb.barrier(("sp",))
        assert kb.check_deadlock()
    return nc


def host_inputs(inputs, b):
    m = {}
    w_in = inputs["w_in"][0]
    m["x"] = np.ascontiguousarray(inputs["x"][b])
    m["pnwT"] = np.ascontiguousarray(inputs["pre_norm_w"][0].reshape(KC, 128).T)
    m["idf"] = np.eye(128, dtype=np.float32)
    wm = np.zeros((NH, D, WM_COLS), np.float32)
    for h in range(NH):
        for g in range(4):
            wm[h, :, g * 128:(g + 1) * 128] = w_in[:, g * 1024 + h * 128: g * 1024 + (h + 1) * 128]
        for g, off in ((0, 512), (1, 544)):
            base = g * 1024 + h * 128
            wm[h, :, off:off + 16] = w_in[:, base + 16: base + 32]
            wm[h, :, off + 16:off + 32] = w_in[:, base: base + 16]
    m["wm"] = np.ascontiguousarray(wm.reshape(NH, KC, 128, WM_COLS).transpose(0, 2, 1, 3))
    half = 16
    inv_freq = np.power(np.float32(ROPE_THETA), -np.arange(half, dtype=np.float32) * np.float32(2.0 / 32)).astype(np.float32)
    ang = (np.arange(S, dtype=np.float32)[:, None] * inv_freq[None, :]).astype(np.float32)
    cos, sin = np.cos(ang).astype(np.float32).T, np.sin(ang).astype(np.float32).T
    m["ropecos"] = np.ascontiguousarray(np.concatenate([cos, cos], 0))
    m["ropesin"] = np.ascontiguousarray(np.concatenate([-sin, sin, -sin, sin], 0))
    m["utmask"] = np.triu(np.ones((128, 128), np.float32))
    m["negt"] = np.where(np.triu(np.ones((128, 128), bool)), 0.0, NEG).astype(np.float32)
    m["smask"] = np.triu(np.ones((128, 128), np.float32), k=1)
    wg = np.zeros((NH, D, 514), np.float32)
    for h in range(NH):
        for g in range(4):
            wg[h, :, g * 128:(g + 1) * 128] = w_in[:, 4096 + g * 1024 + h * 128: 4096 + g * 1024 + (h + 1) * 128]
        wg[h, :, 512] = w_in[:, 8192 + h]
        wg[h, :, 513] = w_in[:, 8200 + h]
    m["wg"] = np.ascontiguousarray(wg.reshape(NH, KC, 128, 514).transpose(0, 2, 1, 3))
    m["cw"] = np.ascontiguousarray(inputs["conv_w"][0].reshape(4, 24, 128).transpose(2, 1, 0))
    m["dtb"] = np.ascontiguousarray(np.broadcast_to(inputs["dt_bias"][0][None, :], (128, NH)))
    m["alog"] = np.ascontiguousarray(np.broadcast_to(inputs["a_log"][0][None, :], (128, NH)))
    m["gnw"] = np.ascontiguousarray(inputs["gdn_norm_w"][0].reshape(128, 1))
    lay = lambda w_: np.ascontiguousarray(w_.reshape(KC, 128, D).transpose(1, 0, 2))
    m["wa"] = lay(inputs["w_branch_a"][0])
    m["wb"] = lay(inputs["w_branch_b"][0])
    m["wo"] = lay(inputs["w_out"][0])
    m["wga"] = lay(w_in[:, 8208:9232])
    m["wgb"] = lay(w_in[:, 9232:10256])
    m["pnw"] = np.ascontiguousarray(np.broadcast_to(inputs["post_norm_w"][0][None, :], (128, D)))
    return m


def kernel(**inputs):
    inputs = {k: np.asarray(v) for k, v in inputs.items()}
    nc = build_program(DEBUG)
    in_maps = [host_inputs(inputs, b) for b in range(8)]
    res = run_bass_kernel_spmd(nc, in_maps, core_ids=list(range(8)))
    if DEBUG.get("raw"):
        return res
    return np.stack([r["y"] for r in res.results], axis=0).astype(np.float32)
```

```python
import contextlib
import numpy as np
import concourse.bass as bass
import concourse.mybir as mybir
from concourse.bass_utils import run_bass_kernel_spmd

F32 = mybir.dt.float32
BF16 = mybir.dt.bfloat16
AF = mybir.ActivationFunctionType
ALU = mybir.AluOpType
AX = mybir.AxisListType

S = 4096
D = 1024
NH = 8
HD = 128
KC = 8
NT = S // 128
NB = S // 512
EPS = 1e-6
NEG = -30000.0
ROPE_THETA = 500000.0

DEBUG = {}


class Buf:
    __slots__ = ("name", "w", "r", "excl")

    def __init__(self, name, excl=False):
        self.name = name
        self.w = None
        self.r = {}
        self.excl = excl


class EngQ:
    def __init__(self, name, eng, sem):
        self.name = name
        self.e = eng
        self.sem = sem
        self.count = 0
        self.seen = {}
        self.events = []


class Chan:
    def __init__(self, name, sem):
        self.name = name
        self.sem = sem
        self.count = 0


class KB:
    def __init__(self, nc, stack):
        self.nc = nc
        self.stack = stack
        self.q = {}
        for name, eng in (("pe", nc.tensor), ("act", nc.scalar), ("dve", nc.vector),
                          ("pool", nc.gpsimd), ("sp", nc.sync)):
            sem = stack.enter_context(nc.semaphore("sem_" + name))
            self.q[name] = EngQ(name, eng, sem)
        self.semq = {id(q.sem): q for q in self.q.values()}
        self.chans = {}
        self.nbuf = 0
        self.nops = 0
        self.cut = None

    def chan(self, name):
        if name not in self.chans:
            sem = self.stack.enter_context(self.nc.semaphore("ch_" + name))
            self.chans[name] = Chan(name, sem)
        return self.chans[name]

    def buf(self, name=None):
        self.nbuf += 1
        return Buf(name or ("b%d" % self.nbuf))

    def bufs(self, n, name="b"):
        return [self.buf("%s%d" % (name, i)) for i in range(n)]

    def _wait(self, q, tk):
        if tk is None:
            return
        sem, val = tk
        key = id(sem)
        if q.seen.get(key, 0) >= val:
            return
        owner = self.semq.get(key)
        if owner is q and q.name == "pe":
            return
        if owner is q and val > q.count:
            return
        q.e.wait_ge(sem, val)
        q.seen[key] = val
        q.events.append(("w", key, val))

    def _deps(self, q, reads, writes):
        for b in reads:
            self._wait(q, b.w)
        for b in writes:
            self._wait(q, b.w)
            for key, tk in list(b.r.items()):
                self._wait(q, tk)

    def _mark(self, tk, reads, writes):
        key = id(tk[0])
        for b in reads:
            old = b.r.get(key)
            if old is None or old[1] < tk[1]:
                b.r[key] = tk
        for b in writes:
            b.w = tk
            b.r = {}

    def barrier(self, names=("pe", "act", "dve", "pool", "sp")):
        for n in names:
            q = self.q[n]
            for o in self.q.values():
                if o is not q and o.count:
                    self._wait(q, (o.sem, o.count))
            for c in self.chans.values():
                if c.count:
                    self._wait(q, (c.sem, c.count))

    def op(self, qn, fn, reads=(), writes=(), sig=True):
        if any(b.excl for b in reads):
            writes = list(writes) + [b for b in reads if b.excl]
            reads = [b for b in reads if not b.excl]
        self.nops += 1
        if self.cut is not None and self.nops > self.cut:
            if not sig:
                return None
            sig = True
            fn = lambda e: e.nop()
            reads, writes = (), ()
        q = self.q[qn]
        self._deps(q, reads, writes)
        ins = fn(q.e)
        tk = (q.sem, q.count + 1)
        if sig:
            ins.then_inc(q.sem, 1)
            q.count += 1
            q.events.append(("i", id(q.sem), 1))
        self._mark(tk, reads, writes)
        return ins

    def dma(self, out, in_, reads=(), writes=(), chan=None, qn="sp", **kw):
        self.nops += 1
        if self.cut is not None and self.nops > self.cut:
            return
        q = self.q[qn]
        ch = self.chan(chan) if isinstance(chan, str) else chan
        self._deps(q, reads, writes)
        self._wait(q, (ch.sem, ch.count))
        q.e.dma_start(out=out, in_=in_, **kw).then_inc(ch.sem, 16)
        ch.count += 16
        q.events.append(("i", id(ch.sem), 16))
        tk = (ch.sem, ch.count)
        self._mark(tk, reads, writes)

    def check_deadlock(self):
        sems = {}
        pos = {n: 0 for n in self.q}
        names = {id(q.sem): "sem_" + q.name for q in self.q.values()}
        names.update({id(c.sem): "ch_" + c.name for c in self.chans.values()})
        progress = True
        while progress:
            progress = False
            for n, q in self.q.items():
                ev = q.events
                while pos[n] < len(ev):
                    kind, key, val = ev[pos[n]]
                    if kind == "w":
                        if sems.get(key, 0) < val:
                            break
                    else:
                        sems[key] = sems.get(key, 0) + val
                    pos[n] += 1
                    progress = True
        stuck = {n: (pos[n], len(q.events)) for n, q in self.q.items() if pos[n] < len(q.events)}
        for n in stuck:
            kind, key, val = self.q[n].events[pos[n]]
            print("DEADLOCK: queue", n, "stuck at event", pos[n], "of", len(self.q[n].events), "waiting", names.get(key), ">=", val,
                  "have", sems.get(key, 0))
        return not stuck

    def finish(self, bufs):
        q = self.q["sp"]
        for b in bufs:
            self._wait(q, b.w)
            for tk in list(b.r.values()):
                self._wait(q, tk)


def mm_group(kb, out_ap, pairs, reads, writes):
    n = len(pairs)
    for i, (l, r) in enumerate(pairs):
        kb.op("pe", lambda e, l=l, r=r, i=i: e.matmul(out_ap, lhsT=l, rhs=r, start=(i == 0), stop=(i == n - 1)),
              reads=reads, writes=writes, sig=(i == n - 1))


WM_COLS = 576


def build_program(dbg=None):
    dbg = dbg or {}
    stop_after = dbg.get("stop_after")
    nc = bass.Bass("TRN2", target_bir_lowering=False)

    def din(name, shape, dt=F32):
        return nc.dram_tensor(name, list(shape), dt, kind="ExternalInput").ap()

    x_d = din("x", [S, D])
    pnwT_d = din("pnwT", [128, KC])
    idf_d = din("idf", [128, 128])
    wm_d = din("wm", [NH, 128, KC, WM_COLS])
    tabc_d = din("ropecos", [32, S])
    tabs_d = din("ropesin", [64, S])
    ut_d = din("utmask", [128, 128])
    negt_d = din("negt", [128, 128])
    smask_d = din("smask", [128, 128])
    wg_d = din("wg", [NH, 128, KC, 514])
    cw_d = din("cw", [128, 24, 4])
    dtb_d = din("dtb", [128, NH])
    alog_d = din("alog", [128, NH])
    gnw_d = din("gnw", [128, 1])
    wa_d = din("wa", [128, KC, D])
    wb_d = din("wb", [128, KC, D])
    wo_d = din("wo", [128, KC, D])
    wga_d = din("wga", [128, KC, D])
    wgb_d = din("wgb", [128, KC, D])
    pnw_d = din("pnw", [128, D])
    y_d = nc.dram_tensor("y", [S, D], F32, kind="ExternalOutput").ap()
    hT_d = nc.dram_tensor("hT_scr", [128, KC, S], BF16, kind="Internal").ap()
    oaT_d = nc.dram_tensor("oaT_scr", [NH, 128, S], BF16, kind="Internal").ap()
    obT_d = nc.dram_tensor("obT_scr", [NH, 128, S], BF16, kind="Internal").ap()
    dbg_out = {}
    for name, shape, dt in dbg.get("outs", []):
        dbg_out[name] = nc.dram_tensor(name, list(shape), dt, kind="ExternalOutput").ap()

    with contextlib.ExitStack() as top:
        kb = KB(nc, top)
        kb.cut = dbg.get("cut")

        def sb(st, name, shape, dt=F32):
            return st.enter_context(nc.sbuf_tensor("s_" + name, list(shape), dt))

        bank = [top.enter_context(nc.psum_tensor("p_bank%d" % i, [128, 512], F32)) for i in range(8)]
        b_bank = kb.bufs(8, "bank")
        for bb in b_bank:
            bb.excl = True

        idf = sb(top, "idf", [128, 128])
        idb = sb(top, "idb", [128, 128], BF16)
        pnwT = sb(top, "pnwT", [128, KC])
        utb = sb(top, "utb", [128, 128], BF16)
        b_const = kb.buf("const")
        kb.dma(idf[:], idf_d[:, :], writes=[b_const], chan="c0")
        kb.dma(pnwT[:], pnwT_d[:, :], writes=[b_const], chan="c1")
        kb.op("dve", lambda e: e.tensor_copy(out=idb[:], in_=idf[:]), reads=[b_const], writes=[b_const])

        b_hTd = kb.bufs(NB, "hTd")
        b_oaTd = [kb.bufs(NB, "oaTd%d_" % h) for h in range(NH)]
        b_obTd = [kb.bufs(NB, "obTd%d_" % h) for h in range(NH)]
        out_bufs = []

        def dump_bf16(name, src_fn, nblk, width, rbufs, oidx=None):
            with contextlib.ExitStack() as pd:
                tmp = sb(pd, "dbg_" + name, [128, width], BF16)
                tmpf = sb(pd, "dbgf_" + name, [128, width], F32)
                b_t = kb.buf("dbgt")
                for i in range(nblk):
                    kb.dma(tmp[:], src_fn(i), reads=[rbufs[i]], writes=[b_t], chan="dbg0")
                    kb.op("dve", lambda e: e.tensor_copy(out=tmpf[:], in_=tmp[:]), reads=[b_t], writes=[b_t])
                    bo = kb.buf("dbgo")
                    kb.dma(dbg_out[name][oidx[i] if oidx else i], tmpf[:], reads=[b_t], writes=[bo], chan="dbg1")
                    out_bufs.append(bo)

        with contextlib.ExitStack() as st_h:
            hT = sb(st_h, "hT", [128, KC, S], BF16)
            b_hT = kb.bufs(NB, "hT")

            with contextlib.ExitStack() as pa:
                xt = [sb(pa, "xt%d" % i, [128, D]) for i in range(2)]
                b_xt = kb.bufs(2, "xt")
                junk = sb(pa, "junkA", [128, D], BF16)
                b_junk = kb.buf("junkA")
                ssq = sb(pa, "ssqA", [128, 2])
                b_ssq = kb.bufs(2, "ssq")
                xn = [sb(pa, "xn%d" % i, [128, D], BF16) for i in range(2)]
                b_xn = kb.bufs(2, "xn")
                if dbg.get("skip_A"):
                    kb.op("dve", lambda e: e.memset(hT[:], 0.5), writes=b_hT)
                for t in range(0 if dbg.get("skip_A") else NT):
                    i = t % 2
                    blk = t // 4
                    pT = bank[i][:].bitcast(BF16).rearrange("p (k n) -> p k n", k=KC)
                    kb.dma(xt[i][:], x_d[t * 128:(t + 1) * 128, :], writes=[b_xt[i]], chan="ldx%d" % i)
                    kb.op("act", lambda e: e.activation(out=junk[:], in_=xt[i][:], func=AF.Square, accum_out=ssq[:, i:i + 1]),
                          reads=[b_xt[i]], writes=[b_junk, b_ssq[i]])
                    kb.op("act", lambda e: e.activation(out=ssq[:, i:i + 1], in_=ssq[:, i:i + 1], func=AF.Sqrt, bias=EPS, scale=1.0 / D),
                          reads=[b_ssq[i]], writes=[b_ssq[i]])
                    kb.op("dve", lambda e: e.reciprocal(out=ssq[:, i:i + 1], in_=ssq[:, i:i + 1]), reads=[b_ssq[i]], writes=[b_ssq[i]])
                    kb.op("dve", lambda e: e.tensor_scalar(out=xn[i][:], in0=xt[i][:], scalar1=ssq[:, i:i + 1], scalar2=None, op0=ALU.mult),
                          reads=[b_xt[i], b_ssq[i]], writes=[b_xn[i]])
                    for kc in range(KC):
                        kb.op("pe", lambda e, kc=kc: e.transpose(pT[:, kc, :], xn[i][:, kc * 128:(kc + 1) * 128], idb[:]),
                              reads=[b_xn[i], b_const], writes=[b_bank[i]], sig=(kc == KC - 1))
                    kb.op("dve", lambda e: e.tensor_tensor(out=hT[:, :, t * 128:(t + 1) * 128], in0=pT,
                                                          in1=pnwT[:].unsqueeze(2).to_broadcast([128, KC, 128]), op=ALU.mult),
                          reads=[b_bank[i], b_const], writes=[b_hT[blk]])
                    if t % 4 == 3:
                        kb.dma(hT_d[:, :, blk * 512:(blk + 1) * 512], hT[:, :, blk * 512:(blk + 1) * 512],
                               reads=[b_hT[blk]], writes=[b_hTd[blk]], chan="sth%d" % (blk % 2))
            if stop_after == "A":
                kb.finish(out_bufs + b_hTd)
                return nc

            if not dbg.get("no_barrier"):
                kb.barrier()
            if dbg.get("print_ops"):
                print("ops before phase B:", kb.nops)
            with contextlib.ExitStack() as pm:
                tabc = sb(pm, "tabc", [128, S])[0:32]
                tabs = sb(pm, "tabs", [128, S])[0:64]
                b_tab = kb.buf("tab")
                if not dbg.get("no_tabs"):
                    kb.dma(tabc[:], tabc_d[:, :], writes=[b_tab], chan="c0")
                    kb.dma(tabs[:], tabs_d[:, :], writes=[b_tab], chan="c0")
                utf = sb(pm, "utf", [128, 128])
                kb.dma(utf[:], ut_d[:, :], writes=[b_const], chan="c1")
                kb.op("dve", lambda e: e.tensor_copy(out=utb[:], in_=utf[:]), reads=[b_const], writes=[b_const])
                wst = sb(pm, "wst", [128, KC, WM_COLS])
                b_wst = kb.buf("wst")
                wb = [sb(pm, "wb%d" % i, [128, KC, WM_COLS], BF16) for i in range(2)]
                b_wb = kb.bufs(2, "wb")
                QT = sb(pm, "QT", [128, S], BF16)
                KT = sb(pm, "KT", [128, S], BF16)
                VA = sb(pm, "VA", [128, NT, 129], BF16)
                SZT = sb(pm, "SZT", [128, S], BF16)
                b_QT, b_KT, b_VA, b_SZT = kb.bufs(NB, "QT"), kb.bufs(NB, "KT"), kb.bufs(NB, "VA"), kb.bufs(NB, "SZT")
                b_KTall = kb.buf("KTall")
                rt1 = sb(pm, "rt1", [128, 512])[0:32]
                rt2 = sb(pm, "rt2", [128, 512])[0:32]
                rt3 = sb(pm, "rt3", [128, 512])[0:64]
                rt4 = sb(pm, "rt4", [128, 512])[0:32]
                b_rt1, b_rt2, b_rt3, b_rt4 = kb.buf("rt1"), kb.buf("rt2"), kb.buf("rt3"), kb.buf("rt4")
                km = sb(pm, "km", [128, 16])
                kmh = sb(pm, "kmh", [128, 16], BF16)
                kml = sb(pm, "kml", [128, 16], BF16)
                b_km = kb.buf("km")
                gs = sb(pm, "gs", [128, 4, 16])
                m8 = sb(pm, "m8", [128, 4, 8])
                sel = sb(pm, "sel", [128, 4, 16])
                b_gs, b_sel = kb.buf("gs"), kb.buf("sel")
                PT = [sb(pm, "PT%d" % i, [128, 512], BF16) for i in range(4)]
                b_PT = kb.bufs(4, "PT")
                oacc = sb(pm, "oacc", [128, 4, 129])
                b_oacc = kb.bufs(4, "oacc")
                rc = sb(pm, "rc", [128, 4])
                onb = sb(pm, "onb", [128, 4, 128], BF16)
                b_onb = kb.bufs(4, "onb")
                oast = [sb(pm, "oast%d" % i, [128, 512], BF16) for i in range(2)]
                b_oast = kb.bufs(2, "oast")
                if not dbg.get("no_memsets"):
                    kb.op("pool", lambda e: e.memset(VA[:, :, 128:129], 1.0), writes=b_VA)

                def load_w(h):
                    kb.dma(wst[:], wm_d[h], writes=[b_wst], chan="ldw")
                    kb.op("pool", lambda e: e.tensor_copy(out=wb[h % 2][:], in_=wst[:]), reads=[b_wst], writes=[b_wb[h % 2]])

                load_w(0)
                pjn = [0]

                def pj_next():
                    i = pjn[0] % 2
                    pjn[0] += 1
                    return i

                SC = float(HD) ** -0.5
                for h in range(dbg.get("moba_heads", NH)):
                    w = wb[h % 2]
                    bw = b_wb[h % 2]
                    if h + 1 < NH and not dbg.get("no_prefetch"):
                        load_w(h + 1)
                    if not dbg.get("no_memsets"):
                        kb.op("pool", lambda e: e.memset(gs[:], -1e30), writes=[b_gs])
                    for tb in range(dbg.get("proj_tb", NB)):
                        ts_ = slice(tb * 512, (tb + 1) * 512)
                        i = pj_next()
                        mm_group(kb, bank[i][:], [(w[:, kc, 0:128], hT[:, kc, ts_]) for kc in range(KC)], [bw, b_hT[tb]], [b_bank[i]])
                        if dbg.get("v478") == "dve":
                            kb.op("dve", lambda e: e.tensor_copy(out=QT[:, ts_], in_=bank[i][:]), reads=[b_bank[i]], writes=[b_QT[tb]])
                        elif dbg.get("v478") == "nopsum":
                            kb.op("act", lambda e: e.copy(out=QT[:, ts_], in_=hT[:, 0, ts_]), reads=[b_bank[i]], writes=[b_QT[tb]])
                        elif dbg.get("v478") == "bank3":
                            kb.op("act", lambda e: e.copy(out=QT[:, ts_], in_=bank[3][:]), reads=[b_bank[i]], writes=[b_QT[tb]])
                        elif dbg.get("v478") == "f32out":
                            kb.op("act", lambda e: e.copy(out=tabc[:, 0:512], in_=bank[i][0:32, :]), reads=[b_bank[i]], writes=[b_QT[tb]])
                        elif dbg.get("v478") == "szt":
                            kb.op("act", lambda e: e.copy(out=SZT[:, ts_], in_=bank[i][:]), reads=[b_bank[i]], writes=[b_QT[tb]])
                        else:
                            kb.op("act", lambda e: e.copy(out=QT[:, ts_], in_=bank[i][:]), reads=[b_bank[i]], writes=[b_QT[tb]])
                        if dbg.get("proj_parts") == "q":
                            continue
                        if not dbg.get("no_rope_dve"):
                            kb.op("dve", lambda e: e.tensor_tensor(out=rt1[:], in0=bank[i][0:32, :], in1=tabc[:, ts_], op=ALU.mult),
                                  reads=[b_bank[i], b_tab] + ([b_QT[tb]] if dbg.get("ser") else []), writes=[b_rt1])
                        i = pj_next()
                        mm_group(kb, bank[i][:], [(w[:, kc, 128:256], hT[:, kc, ts_]) for kc in range(KC)], [bw, b_hT[tb]], [b_bank[i]])
                        kb.op("act", lambda e: e.copy(out=KT[:, ts_], in_=bank[i][:]), reads=[b_bank[i]], writes=[b_KT[tb], b_KTall])
                        if not dbg.get("no_rope_dve"):
                            kb.op("dve", lambda e: e.tensor_tensor(out=rt2[:], in0=bank[i][0:32, :], in1=tabc[:, ts_], op=ALU.mult),
                                  reads=[b_bank[i], b_tab] + ([b_KT[tb]] if dbg.get("ser") else []), writes=[b_rt2])
                        if dbg.get("proj_parts") == "qk":
                            continue
                        i = pj_next()
                        mm_group(kb, bank[i][0:64, :], [(w[:, kc, 512:576], hT[:, kc, ts_]) for kc in range(KC)], [bw, b_hT[tb]], [b_bank[i]])
                        kb.op("dve", lambda e: e.tensor_tensor(out=rt3[:], in0=bank[i][0:64, :], in1=tabs[:, ts_], op=ALU.mult),
                              reads=[b_bank[i], b_tab], writes=[b_rt3])
                        kb.op("pool", lambda e: e.tensor_copy(out=rt4[:], in_=rt3[32:64, :]), reads=[b_rt3], writes=[b_rt4])
                        kb.op("pool", lambda e: e.tensor_tensor(out=QT[0:32, ts_], in0=rt1[:], in1=rt3[0:32, :], op=ALU.add),
                              reads=[b_rt1, b_rt3], writes=[b_QT[tb]])
                        kb.op("pool", lambda e: e.tensor_tensor(out=KT[0:32, ts_], in0=rt2[:], in1=rt4[:], op=ALU.add),
                              reads=[b_rt2, b_rt4], writes=[b_KT[tb], b_KTall])
                        if dbg.get("proj_parts") == "qks":
                            continue
                        i = pj_next()
                        for sub in range(4):
                            tk = slice(tb * 512 + sub * 128, tb * 512 + (sub + 1) * 128)
                            mm_group(kb, bank[i][:, sub * 128:(sub + 1) * 128], [(hT[:, kc, tk], w[:, kc, 256:384]) for kc in range(KC)],
                                     [bw, b_hT[tb]], [b_bank[i]])
                        kb.op("act", lambda e: e.copy(out=VA[:, tb * 4:(tb + 1) * 4, 0:128], in_=bank[i][:].rearrange("p (a b) -> p a b", a=4)),
                              reads=[b_bank[i]], writes=[b_VA[tb]])
                        if dbg.get("proj_parts") == "qksv":
                            continue
                        i = pj_next()
                        mm_group(kb, bank[i][:], [(w[:, kc, 384:512], hT[:, kc, ts_]) for kc in range(KC)], [bw, b_hT[tb]], [b_bank[i]])
                        kb.op("act", lambda e: e.activation(out=SZT[:, ts_], in_=bank[i][:], func=AF.Silu), reads=[b_bank[i]], writes=[b_SZT[tb]])
                    if dbg.get("print_ops"):
                        print("ops after proj head", h, kb.nops)
                    if dbg.get("proj_only"):
                        continue
                    kb.op("dve", lambda e: e.tensor_reduce(out=km[:], in_=KT[:].rearrange("p (n l) -> p n l", l=256), axis=AX.X, op=ALU.add),
                          reads=[b_KTall], writes=[b_km])
                    kb.op("act", lambda e: e.activation(out=kmh[:], in_=km[:], func=AF.Copy, scale=1.0 / 256), reads=[b_km], writes=[b_km])
                    kb.op("dve", lambda e: e.scalar_tensor_tensor(out=kml[:], in0=km[:], scalar=1.0 / 256, in1=kmh[:], op0=ALU.mult, op1=ALU.subtract),
                          reads=[b_km], writes=[b_km])
                    pacc_set = 0
                    psn = 0
                    ptn = 0
                    for st in range(dbg.get("moba_st", NB)):
                        b0, b1 = 2 * st, 2 * st + 1
                        qblk = [b0, b0, b1, b1]
                        qs = slice(st * 512, (st + 1) * 512)
                        use_sel = b1 >= 4
                        if use_sel:
                            pG = bank[7][:, 0:64].rearrange("p (a b) -> p a b", a=4)
                            for j in range(4):
                                qsl = slice(st * 512 + j * 128, st * 512 + (j + 1) * 128)
                                for ii, kmx in enumerate((kmh, kml)):
                                    kb.op("pe", lambda e, kmx=kmx, ii=ii: e.matmul(pG[:, j, :], lhsT=QT[:, qsl], rhs=kmx[:], start=(ii == 0), stop=(ii == 1)),
                                          reads=[b_QT[st], b_km], writes=[b_bank[7]], sig=(j == 3 and ii == 1))
                            for (ja, jb, bb) in ((0, 2, b0), (2, 4, b1)):
                                if bb >= 1:
                                    kb.op("dve", lambda e: e.tensor_copy(out=gs[:, ja:jb, 0:bb], in_=pG[:, ja:jb, 0:bb]), reads=[b_bank[7]], writes=[b_gs])
                            for j in range(4):
                                kb.op("dve", lambda e: e.max(out=m8[:, j, :], in_=gs[:, j, :]), reads=[b_gs], writes=[b_sel])
                            kb.op("dve", lambda e: e.tensor_tensor(out=sel[:], in0=gs[:], in1=m8[:, :, 2:3].to_broadcast([128, 4, 16]), op=ALU.is_ge),
                                  reads=[b_gs, b_sel], writes=[b_sel])
                        first = [True] * 4
                        for n in range(b1 + 1):
                            jlo = 0 if n <= b0 else 2
                            seta = (4, 5) if pacc_set == 0 else (0, 1)
                            pacc_set ^= 1
                            pacc = {}
                            for j in range(4):
                                bk = seta[j // 2]
                                pacc[j] = (bank[bk][:, (j % 2) * 129:(j % 2 + 1) * 129], b_bank[bk])
                            pts = {}
                            for c in range(2):
                                kt = 2 * n + c
                                jv = max(jlo, kt - 4 * st)
                                if jv > 3:
                                    continue
                                cols = slice(jv * 128, 512)
                                bS = 2 + (psn % 2)
                                psn += 1
                                kb.op("pe", lambda e: e.matmul(bank[bS][:, cols], lhsT=KT[:, kt * 128:(kt + 1) * 128], rhs=QT[:, st * 512 + jv * 128:(st + 1) * 512],
                                                               start=True, stop=True),
                                      reads=[b_KT[kt // 4], b_QT[st]], writes=[b_bank[bS]])
                                p = ptn % 4
                                ptn += 1
                                pts[c] = (p, jv, kt)
                                kb.op("act", lambda e: e.activation(out=PT[p][:, cols], in_=bank[bS][:, cols], func=AF.Exp, scale=SC),
                                      reads=[b_bank[bS]], writes=[b_PT[p]])
                                if kt >= 4 * st and kt - 4 * st == jv:
                                    dsl = slice(jv * 128, (jv + 1) * 128)
                                    kb.op("pool", lambda e: e.tensor_tensor(out=PT[p][:, dsl], in0=PT[p][:, dsl], in1=utb[:], op=ALU.mult),
                                          reads=[b_PT[p], b_const], writes=[b_PT[p]])
                            for j in range(jlo, 4):
                                ap, bb_ = pacc[j]
                                vis = [c for c in pts if pts[c][1] <= j]
                                for ci, c in enumerate(vis):
                                    p, jv, kt = pts[c]
                                    kb.op("pe", lambda e: e.matmul(ap, lhsT=PT[p][:, j * 128:(j + 1) * 128], rhs=VA[:, kt, :],
                                                                   start=(ci == 0), stop=(ci == len(vis) - 1)),
                                          reads=[b_PT[p], b_VA[kt // 4]], writes=[bb_], sig=(ci == len(vis) - 1))
                            for j in range(jlo, 4):
                                ap, bb_ = pacc[j]
                                own = (n == qblk[j])
                                if own or not use_sel or qblk[j] < 4:
                                    if first[j]:
                                        kb.op("dve", lambda e: e.tensor_copy(out=oacc[:, j, :], in_=ap), reads=[bb_], writes=[b_oacc[j]])
                                    else:
                                        kb.op("dve", lambda e: e.tensor_tensor(out=oacc[:, j, :], in0=ap, in1=oacc[:, j, :], op=ALU.add),
                                              reads=[bb_, b_oacc[j]], writes=[b_oacc[j]])
                                else:
                                    if first[j]:
                                        kb.op("dve", lambda e: e.tensor_scalar(out=oacc[:, j, :], in0=ap, scalar1=sel[:, j, n:n + 1], scalar2=None, op0=ALU.mult),
                                              reads=[bb_, b_sel], writes=[b_oacc[j]])
                                    else:
                                        kb.op("dve", lambda e: e.scalar_tensor_tensor(out=oacc[:, j, :], in0=ap, scalar=sel[:, j, n:n + 1], in1=oacc[:, j, :],
                                                                                     op0=ALU.mult, op1=ALU.add),
                                              reads=[bb_, b_sel, b_oacc[j]], writes=[b_oacc[j]])
                                first[j] = False
                        pTv = bank[6][:].bitcast(BF16)[:, 0:512].rearrange("p (a b) -> p a b", a=4)
                        for j in range(4):
                            kb.op("dve", lambda e: e.reciprocal(out=rc[:, j:j + 1], in_=oacc[:, j, 128:129]), reads=[b_oacc[j]], writes=[b_onb[j]])
                            kb.op("dve", lambda e: e.tensor_scalar(out=onb[:, j, :], in0=oacc[:, j, 0:128], scalar1=rc[:, j:j + 1], scalar2=None, op0=ALU.mult),
                                  reads=[b_oacc[j], b_onb[j]], writes=[b_onb[j]])
                            kb.op("pe", lambda e: e.transpose(pTv[:, j, :], onb[:, j, :], idb[:]), reads=[b_onb[j], b_const], writes=[b_bank[6]], sig=(j == 3))
                        o = st % 2
                        kb.op("dve", lambda e: e.tensor_tensor(out=oast[o][:], in0=bank[6][:].bitcast(BF16)[:, 0:512], in1=SZT[:, qs], op=ALU.mult),
                              reads=[b_bank[6], b_SZT[st]], writes=[b_oast[o]])
                        kb.dma(oaT_d[h][:, qs], oast[o][:], reads=[b_oast[o]], writes=[b_oaTd[h][st]], chan="sto%d" % o)
            if "oaT" in dbg_out:
                idx = [h * NB + st for h in range(dbg.get("moba_heads", NH)) for st in range(dbg.get("moba_st", NB))]
                dump_bf16("oaT", lambda i: oaT_d[idx[i] // NB][:, (idx[i] % NB) * 512:(idx[i] % NB + 1) * 512], len(idx), 512,
                          [b_oaTd[i // NB][i % NB] for i in idx], idx)
            if stop_after == "B":
                kb.barrier(("sp",))
                assert kb.check_deadlock()
                return nc

            kb.barrier()
            with contextlib.ExitStack() as pg:
                utf2 = sb(pg, "utf2", [128, 128])
                negt = sb(pg, "negt", [128, 128])
                smask = sb(pg, "smask", [128, 128])
                ones = sb(pg, "ones", [128, 128])
                cw = sb(pg, "cw", [128, 24, 4])
                dtb = sb(pg, "dtb", [128, NH])
                nega = sb(pg, "nega", [128, NH])
                gnw = sb(pg, "gnw", [128, 1])
                b_gc = kb.buf("gconst")
                for t_, d_ in ((utf2, ut_d), (negt, negt_d), (smask, smask_d), (cw, cw_d), (dtb, dtb_d), (nega, alog_d), (gnw, gnw_d)):
                    kb.dma(t_[:], d_, writes=[b_gc], chan="c0")
                kb.op("pool", lambda e: e.memset(ones[:], 1.0), writes=[b_gc])
                kb.op("act", lambda e: e.activation(out=nega[:], in_=nega[:], func=AF.Exp), reads=[b_gc], writes=[b_gc])
                kb.op("dve", lambda e: e.tensor_scalar(out=nega[:], in0=nega[:], scalar1=-1.0, scalar2=None, op0=ALU.mult), reads=[b_gc], writes=[b_gc])
                wgs = sb(pg, "wgs", [128, KC, 514])
                b_wgs = kb.buf("wgs")
                wgb = [sb(pg, "wgb0", [128, KC, 514], BF16)] * 2
                b_wgb = [kb.buf("wgb")] * 2
                RAW = sb(pg, "RAW", [128, S + 3])
                b_RAW = kb.buf("RAW")
                qkv = [sb(pg, "gq", [128, S]), sb(pg, "gk", [128, S]), sb(pg, "gv", [128, S])]
                b_qkv = kb.bufs(3, "gqkv")
                szb = sb(pg, "szb", [128, S], BF16)
                b_szb = kb.buf("szb")
                tsq = sb(pg, "tsq", [128, 512])
                rn = sb(pg, "rn", [128, 512])
                b_tsq, b_rn = kb.buf("tsq"), kb.buf("rn")
                NU = S // 128
                sm = {n: sb(pg, "g_" + n, [128, NU]) for n in ("beta", "negb", "sp", "gall", "Gtok", "negG", "eG", "kdsc", "glast")}
                b_sm = kb.buf("gsmall")

                def t128(name, w=128, dt=F32):
                    return sb(pg, "u_" + name, [128, w], dt), kb.buf("u_" + name)

                NSLOT = 4
                slots = []
                for s_ in range(NSLOT):
                    slots.append({nm: t128("%s_%d" % (nm, s_), w_) for nm, w_ in (("diagG", 128), ("decT", 128), ("decTs", 128), ("QKdT", 128), ("PQ0", 256),
                                                                                  ("PQ1", 256), ("X0", 128), ("X1", 128), ("keG", 128), ("kdec", 128),
                                                                                  ("vtok", 128), ("bU", 128), ("WT", 128))})
                vnew, b_vnew = t128("vnew")
                SS = [t128("S%d" % i) for i in range(2)]
                t1, b_t1 = t128("t1")
                oo, b_oo = t128("oo")
                on_, b_on = t128("on")
                junkg, b_junkg = t128("junkg")
                ms, b_ms = t128("ms", 2)
                obst = [sb(pg, "obst%d" % i, [128, 512], BF16) for i in range(2)]
                b_obst = kb.bufs(2, "obst")
                kb.op("pool", lambda e: e.memset(RAW[:, 0:3], 0.0), writes=[b_RAW])

                def load_wg(h):
                    kb.dma(wgs[:], wg_d[h], writes=[b_wgs], chan="ldw")

                def cast_wg(h):
                    kb.op("pool", lambda e: e.tensor_copy(out=wgb[0][:], in_=wgs[:]), reads=[b_wgs], writes=[b_wgb[0]])

                n_gh = dbg.get("gdn_heads", NH)
                n_gu = dbg.get("gdn_units", NU)
                load_wg(0)
                for h in range(n_gh):
                    w = wgb[0]
                    bw = b_wgb[0]
                    cast_wg(h)
                    if h + 1 < n_gh:
                        load_wg(h + 1)
                    def proj_conv(c):
                        dst, bd = qkv[c], b_qkv[c]
                        for tb in range(NB):
                            ts_ = slice(tb * 512, (tb + 1) * 512)
                            i = pj_next()
                            mm_group(kb, bank[i][:], [(w[:, kc, c * 128:(c + 1) * 128], hT[:, kc, ts_]) for kc in range(KC)], [bw, b_hT[tb]], [b_bank[i]])
                            kb.op("act", lambda e: e.copy(out=RAW[:, 3 + tb * 512:3 + (tb + 1) * 512], in_=bank[i][:]), reads=[b_bank[i]], writes=[b_RAW])
                        ct = c * NH + h
                        kb.op("act", lambda e: e.activation(out=dst[:], in_=RAW[:, 3:S + 3], func=AF.Copy, scale=cw[:, ct, 3:4]),
                              reads=[b_RAW, b_gc], writes=[bd])
                        for j in (2, 1, 0):
                            kb.op("dve", lambda e, j=j: e.scalar_tensor_tensor(out=dst[:], in0=RAW[:, j:S + j], scalar=cw[:, ct, j:j + 1], in1=dst[:],
                                                                            op0=ALU.mult, op1=ALU.add),
                                  reads=[b_RAW, b_gc, bd], writes=[bd])
                        kb.op("act", lambda e: e.activation(out=dst[:], in_=dst[:], func=AF.Silu), reads=[bd], writes=[bd])

                    def l2n(c):
                        dst, bd = qkv[c], b_qkv[c]
                        for tb in range(NB):
                            ts_ = slice(tb * 512, (tb + 1) * 512)
                            kb.op("pool", lambda e: e.tensor_tensor(out=tsq[:], in0=dst[:, ts_], in1=dst[:, ts_], op=ALU.mult), reads=[bd], writes=[b_tsq])
                            i = pj_next()
                            kb.op("pe", lambda e: e.matmul(bank[i][:], lhsT=ones[:], rhs=tsq[:], start=True, stop=True), reads=[b_tsq, b_gc], writes=[b_bank[i]])
                            scl = float(HD) if c == 0 else 1.0
                            kb.op("act", lambda e: e.activation(out=rn[:], in_=bank[i][:], func=AF.Sqrt, bias=EPS * scl, scale=scl), reads=[b_bank[i]], writes=[b_rn])
                            kb.op("dve", lambda e: e.reciprocal(out=rn[:], in_=rn[:]), reads=[b_rn], writes=[b_rn])
                            kb.op("dve", lambda e: e.tensor_tensor(out=dst[:, ts_], in0=dst[:, ts_], in1=rn[:], op=ALU.mult), reads=[bd, b_rn], writes=[bd])

                    proj_conv(0)
                    proj_conv(1)
                    l2n(0)
                    proj_conv(2)
                    l2n(1)
                    for tb in range(NB):
                        ts_ = slice(tb * 512, (tb + 1) * 512)
                        i = pj_next()
                        mm_group(kb, bank[i][:], [(w[:, kc, 384:512], hT[:, kc, ts_]) for kc in range(KC)], [bw, b_hT[tb]], [b_bank[i]])
                        kb.op("act", lambda e: e.activation(out=szb[:, ts_], in_=bank[i][:], func=AF.Silu), reads=[b_bank[i]], writes=[b_szb])
                    lg = bank[2][:, 0:2 * NU].rearrange("p (u c) -> p u c", c=2)
                    for u in range(NU):
                        for kc in range(KC):
                            kb.op("pe", lambda e, kc=kc: e.matmul(lg[:, u, :], lhsT=hT[:, kc, u * 128:(u + 1) * 128], rhs=w[:, kc, 512:514],
                                                                 start=(kc == 0), stop=(kc == KC - 1)),
                                  reads=[bw, b_hT[u // 4]], writes=[b_bank[2]], sig=(u == NU - 1 and kc == KC - 1))
                    kb.op("act", lambda e: e.activation(out=sm["beta"][:], in_=lg[:, :, 0], func=AF.Sigmoid), reads=[b_bank[2]], writes=[b_sm])
                    kb.op("act", lambda e: e.activation(out=sm["sp"][:], in_=lg[:, :, 1], func=AF.Exp, bias=dtb[:, h:h + 1]), reads=[b_bank[2], b_gc], writes=[b_sm])
                    kb.op("act", lambda e: e.activation(out=sm["sp"][:], in_=sm["sp"][:], func=AF.Ln, bias=1.0), reads=[b_sm], writes=[b_sm])
                    kb.op("dve", lambda e: e.tensor_scalar(out=sm["gall"][:], in0=sm["sp"][:], scalar1=nega[:, h:h + 1], scalar2=None, op0=ALU.mult),
                          reads=[b_sm, b_gc], writes=[b_sm])
                    kb.op("dve", lambda e: e.tensor_scalar(out=sm["negb"][:], in0=sm["beta"][:], scalar1=-1.0, scalar2=None, op0=ALU.mult), reads=[b_sm], writes=[b_sm])
                    kb.op("pe", lambda e: e.matmul(bank[2][:, 64:64 + NU], lhsT=utf2[:], rhs=sm["gall"][:], start=True, stop=True), reads=[b_sm, b_gc], writes=[b_bank[2]], sig=False)
                    kb.op("pe", lambda e: e.matmul(bank[2][:, 128:128 + NU], lhsT=ones[:], rhs=sm["gall"][:], start=True, stop=True), reads=[b_sm, b_gc], writes=[b_bank[2]])
                    kb.op("dve", lambda e: e.tensor_copy(out=sm["Gtok"][:], in_=bank[2][:, 64:64 + NU]), reads=[b_bank[2]], writes=[b_sm])
                    kb.op("dve", lambda e: e.tensor_scalar(out=sm["negG"][:], in0=sm["Gtok"][:], scalar1=-1.0, scalar2=None, op0=ALU.mult), reads=[b_sm], writes=[b_sm])
                    kb.op("act", lambda e: e.activation(out=sm["glast"][:], in_=bank[2][:, 128:128 + NU], func=AF.Exp), reads=[b_bank[2]], writes=[b_sm])
                    kb.op("dve", lambda e: e.tensor_tensor(out=sm["kdsc"][:], in0=bank[2][:, 128:128 + NU], in1=sm["Gtok"][:], op=ALU.subtract), reads=[b_bank[2], b_sm], writes=[b_sm])
                    kb.op("act", lambda e: e.activation(out=sm["kdsc"][:], in_=sm["kdsc"][:], func=AF.Exp), reads=[b_sm], writes=[b_sm])
                    kb.op("act", lambda e: e.activation(out=sm["eG"][:], in_=sm["Gtok"][:], func=AF.Exp), reads=[b_sm], writes=[b_sm])
                    kb.op("pool", lambda e: e.memset(SS[0][0][:], 0.0), writes=[SS[0][1]])
                    qT_, kT_, vT_ = qkv

                    def solve(u):
                        sl = slots[u % NSLOT]
                        bA, bB = 2 + (u % NSLOT), 6 + (u % NSLOT) // 2
                        yo = 256 * ((u % NSLOT) % 2)
                        us = slice(u * 128, (u + 1) * 128)
                        col = lambda n: sm[n][:, u:u + 1]
                        diagG, b_diagG = sl["diagG"]
                        decT, b_decT = sl["decT"]
                        decTs, b_decTs = sl["decTs"]
                        QKdT, b_QKdT = sl["QKdT"]
                        PQ = [sl["PQ0"], sl["PQ1"]]
                        XX = [sl["X0"], sl["X1"]]
                        keG, b_keG = sl["keG"]
                        kdec, b_kdec = sl["kdec"]
                        vtok, b_vtok = sl["vtok"]
                        bU, b_bU = sl["bU"]
                        WT, b_WT = sl["WT"]
                        kb.op("dve", lambda e: e.tensor_scalar(out=diagG[:], in0=idf[:], scalar1=col("Gtok"), scalar2=None, op0=ALU.mult),
                              reads=[b_const, b_sm], writes=[b_diagG])
                        kb.op("pe", lambda e: e.matmul(bank[bA][:, 0:128], lhsT=ones[:], rhs=diagG[:], start=True, stop=False), reads=[b_diagG, b_gc], writes=[b_bank[bA]], sig=False)
                        kb.op("pe", lambda e: e.matmul(bank[bA][:, 0:128], lhsT=idf[:], rhs=negt[:], start=False, stop=True), reads=[b_const, b_gc], writes=[b_bank[bA]])
                        kb.op("pe", lambda e: e.matmul(bank[bB][:, yo:yo + 128], lhsT=kT_[:, us], rhs=kT_[:, us], start=True, stop=True), reads=[b_qkv[1]], writes=[b_bank[bB]], sig=False)
                        kb.op("pe", lambda e: e.matmul(bank[bB][:, yo + 128:yo + 256], lhsT=kT_[:, us], rhs=qT_[:, us], start=True, stop=True), reads=[b_qkv[1], b_qkv[0]], writes=[b_bank[bB]])
                        yield
                        kb.op("act", lambda e: e.activation(out=decT[:], in_=bank[bA][:, 0:128], func=AF.Exp, bias=col("negG")), reads=[b_bank[bA], b_sm], writes=[b_decT])
                        yield
                        kb.op("pool", lambda e: e.tensor_tensor(out=decTs[:], in0=decT[:], in1=smask[:], op=ALU.mult), reads=[b_decT, b_gc], writes=[b_decTs])
                        kb.op("dve", lambda e: e.tensor_tensor(out=QKdT[:], in0=bank[bB][:, yo + 128:yo + 256], in1=decT[:], op=ALU.mult), reads=[b_bank[bB], b_decT], writes=[b_QKdT])
                        yield
                        Q0, bQ0 = PQ[0]
                        kb.op("dve", lambda e: e.scalar_tensor_tensor(out=Q0[:, 128:256], in0=bank[bB][:, yo:yo + 128], scalar=col("negb"), in1=decTs[:],
                                                                     op0=ALU.mult, op1=ALU.mult),
                              reads=[b_bank[bB], b_sm, b_decTs], writes=[bQ0])
                        yield
                        kb.op("pe", lambda e: e.transpose(bank[bA][:, 256:384], Q0[:, 128:256], idf[:]), reads=[bQ0, b_const], writes=[b_bank[bA]])
                        kb.op("pe", lambda e: e.transpose(bank[bB][:, yo:yo + 128], kT_[:, us], idf[:]), reads=[b_qkv[1], b_const], writes=[b_bank[bB]], sig=False)
                        kb.op("pe", lambda e: e.transpose(bank[bB][:, yo + 128:yo + 256], vT_[:, us], idf[:]), reads=[b_qkv[2], b_const], writes=[b_bank[bB]])
                        yield
                        kb.op("act", lambda e: e.copy(out=Q0[:, 0:128], in_=bank[bA][:, 256:384]), reads=[b_bank[bA]], writes=[bQ0])
                        X0, bX0 = XX[0]
                        kb.op("pool", lambda e: e.tensor_tensor(out=X0[:], in0=Q0[:, 128:256], in1=idf[:], op=ALU.add), reads=[bQ0, b_const], writes=[bX0])
                        kb.op("act", lambda e: e.activation(out=keG[:], in_=bank[bB][:, yo:yo + 128], func=AF.Copy, scale=col("eG")), reads=[b_bank[bB], b_sm], writes=[b_keG])
                        kb.op("dve", lambda e: e.tensor_scalar(out=kdec[:], in0=bank[bB][:, yo:yo + 128], scalar1=col("kdsc"), scalar2=None, op0=ALU.mult),
                              reads=[b_bank[bB], b_sm], writes=[b_kdec])
                        kb.op("act", lambda e: e.copy(out=vtok[:], in_=bank[bB][:, yo + 128:yo + 256]), reads=[b_bank[bB]], writes=[b_vtok])
                        yield
                        xi = 0
                        for lv in range(1, 7):
                            Pp, bPp = PQ[(lv - 1) % 2]
                            Pn, bPn = PQ[lv % 2]
                            kb.op("pe", lambda e: e.matmul(bank[bA][:, 256:384], lhsT=Pp[:, 128:256], rhs=Pp[:, 0:128], start=True, stop=True),
                                  reads=[bPp], writes=[b_bank[bA]], sig=(lv == 6))
                            if lv < 6:
                                kb.op("pe", lambda e: e.matmul(bank[bA][:, 384:512], lhsT=Pp[:, 0:128], rhs=Pp[:, 128:256], start=True, stop=True),
                                      reads=[bPp], writes=[b_bank[bA]])
                            yield
                            if lv < 6:
                                kb.op("act", lambda e: e.copy(out=Pn[:], in_=bank[bA][:, 256:512]), reads=[b_bank[bA]], writes=[bPn])
                            else:
                                kb.op("act", lambda e: e.copy(out=Pn[:, 0:128], in_=bank[bA][:, 256:384]), reads=[b_bank[bA]], writes=[bPn])
                            yield
                            Xc, bXc = XX[xi]
                            Xn, bXn = XX[1 - xi]
                            kb.op("pe", lambda e: e.matmul(bank[bA][:, 128:256], lhsT=Pn[:, 0:128], rhs=Xc[:], start=True, stop=True), reads=[bPn, bXc], writes=[b_bank[bA]])
                            yield
                            kb.op("dve", lambda e: e.tensor_tensor(out=Xn[:], in0=bank[bA][:, 128:256], in1=Xc[:], op=ALU.add), reads=[b_bank[bA], bXc], writes=[bXn])
                            xi = 1 - xi
                            yield
                        Xf, bXf = XX[xi]
                        kb.op("pe", lambda e: e.matmul(bank[bB][:, yo:yo + 128], lhsT=Xf[:], rhs=vtok[:], start=True, stop=True), reads=[bXf, b_vtok], writes=[b_bank[bB]], sig=False)
                        kb.op("pe", lambda e: e.matmul(bank[bB][:, yo + 128:yo + 256], lhsT=keG[:], rhs=Xf[:], start=True, stop=True), reads=[bXf, b_keG], writes=[b_bank[bB]])
                        yield
                        kb.op("dve", lambda e: e.tensor_scalar(out=bU[:], in0=bank[bB][:, yo:yo + 128], scalar1=col("beta"), scalar2=None, op0=ALU.mult),
                              reads=[b_bank[bB], b_sm], writes=[b_bU])
                        kb.op("act", lambda e: e.copy(out=WT[:], in_=bank[bB][:, yo + 128:yo + 256]), reads=[b_bank[bB]], writes=[b_WT])

                    def scan(u):
                        sl = slots[u % NSLOT]
                        us = slice(u * 128, (u + 1) * 128)
                        col = lambda n: sm[n][:, u:u + 1]
                        QKdT, b_QKdT = sl["QKdT"]
                        kdec, b_kdec = sl["kdec"]
                        bU, b_bU = sl["bU"]
                        WT, b_WT = sl["WT"]
                        Sc, bSc = SS[u % 2]
                        Sn, bSn = SS[(u + 1) % 2]
                        kb.op("pe", lambda e: e.matmul(bank[0][:, 0:128], lhsT=WT[:], rhs=Sc[:], start=True, stop=True), reads=[b_WT, bSc], writes=[b_bank[0]], sig=False)
                        kb.op("pe", lambda e: e.matmul(bank[0][:, 128:256], lhsT=qT_[:, us], rhs=Sc[:], start=True, stop=True), reads=[b_qkv[0], bSc], writes=[b_bank[0]])
                        yield
                        kb.op("dve", lambda e: e.scalar_tensor_tensor(out=vnew[:], in0=bank[0][:, 0:128], scalar=col("negb"), in1=bU[:], op0=ALU.mult, op1=ALU.add),
                              reads=[b_bank[0], b_sm, b_bU], writes=[b_vnew])
                        kb.op("act", lambda e: e.activation(out=t1[:], in_=bank[0][:, 128:256], func=AF.Copy, scale=col("eG")), reads=[b_bank[0], b_sm], writes=[b_t1])
                        yield
                        kb.op("pe", lambda e: e.matmul(bank[1][:, 128:256], lhsT=kdec[:], rhs=vnew[:], start=True, stop=True), reads=[b_kdec, b_vnew], writes=[b_bank[1]], sig=False)
                        kb.op("pe", lambda e: e.matmul(bank[1][:, 0:128], lhsT=QKdT[:], rhs=vnew[:], start=True, stop=True), reads=[b_QKdT, b_vnew], writes=[b_bank[1]])
                        yield
                        kb.op("dve", lambda e: e.scalar_tensor_tensor(out=Sn[:], in0=Sc[:], scalar=col("glast"), in1=bank[1][:, 128:256], op0=ALU.mult, op1=ALU.add),
                              reads=[bSc, b_sm, b_bank[1]], writes=[bSn])
                        kb.op("dve", lambda e: e.tensor_tensor(out=oo[:], in0=bank[1][:, 0:128], in1=t1[:], op=ALU.add), reads=[b_bank[1], b_t1], writes=[b_oo])
                        yield
                        kb.op("act", lambda e: e.activation(out=junkg[:], in_=oo[:], func=AF.Square, accum_out=ms[:, 0:1]), reads=[b_oo], writes=[b_junkg, b_ms])
                        kb.op("act", lambda e: e.activation(out=ms[:, 0:1], in_=ms[:, 0:1], func=AF.Sqrt, bias=EPS, scale=1.0 / HD), reads=[b_ms], writes=[b_ms])
                        yield
                        kb.op("dve", lambda e: e.reciprocal(out=ms[:, 0:1], in_=ms[:, 0:1]), reads=[b_ms], writes=[b_ms])
                        kb.op("dve", lambda e: e.tensor_scalar(out=on_[:], in0=oo[:], scalar1=ms[:, 0:1], scalar2=None, op0=ALU.mult), reads=[b_oo, b_ms], writes=[b_on])
                        yield
                        kb.op("pe", lambda e: e.transpose(bank[1][:, 256:384], on_[:], idf[:]), reads=[b_on, b_const], writes=[b_bank[1]])
                        yield
                        ob_i = (u // 4) % 2
                        kb.op("dve", lambda e: e.scalar_tensor_tensor(out=obst[ob_i][:, (u % 4) * 128:(u % 4 + 1) * 128], in0=bank[1][:, 256:384], scalar=gnw[:, 0:1],
                                                                     in1=szb[:, us], op0=ALU.mult, op1=ALU.mult),
                              reads=[b_bank[1], b_gc, b_szb], writes=[b_obst[ob_i]])
                        if u % 4 == 3:
                            blk = u // 4
                            kb.dma(obT_d[h][:, blk * 512:(blk + 1) * 512], obst[ob_i][:], reads=[b_obst[ob_i]], writes=[b_obTd[h][blk]], chan="sto%d" % ob_i)

                    active, done_solve = [], set()
                    su, cu, scan_g = 0, 0, None
                    while cu < n_gu:
                        while su < n_gu and su < cu + NSLOT and len(active) < NSLOT:
                            active.append((su, solve(su)))
                            su += 1
                        for item in list(active):
                            try:
                                next(item[1])
                            except StopIteration:
                                active.remove(item)
                                done_solve.add(item[0])
                        if scan_g is None and cu in done_solve:
                            scan_g = scan(cu)
                        if scan_g is not None:
                            try:
                                next(scan_g)
                            except StopIteration:
                                scan_g = None
                                cu += 1
            if "obT" in dbg_out:
                idx = [h * NB + st for h in range(dbg.get("gdn_heads", NH)) for st in range(dbg.get("gdn_units", S // 128) // 4)]
                dump_bf16("obT", lambda i: obT_d[idx[i] // NB][:, (idx[i] % NB) * 512:(idx[i] % NB + 1) * 512], len(idx), 512,
                          [b_obTd[i // NB][i % NB] for i in idx], idx)
            if stop_after == "C":
                kb.barrier(("sp",))
                assert kb.check_deadlock()
                return nc


        kb.barrier()
        with contextlib.ExitStack() as pe_:
            wstE = sb(pe_, "wstE", [128, KC, 512])
            b_wstE = kb.buf("wstE")
            wE = {n: sb(pe_, "wE_" + n, [128, KC, D], BF16) for n in ("wa", "wb", "wo", "wga", "wgb")}
            b_wE = kb.buf("wE")
            for n, d_ in (("wa", wa_d), ("wb", wb_d), ("wo", wo_d), ("wga", wga_d), ("wgb", wgb_d)):
                for hf in range(2):
                    kb.dma(wstE[:], d_[:, :, hf * 512:(hf + 1) * 512], writes=[b_wstE], chan="ldw")
                    kb.op("pool", lambda e: e.tensor_copy(out=wE[n][:, :, hf * 512:(hf + 1) * 512], in_=wstE[:]), reads=[b_wstE], writes=[b_wE])
            pnw = sb(pe_, "pnw", [128, D])
            kb.dma(pnw[:], pnw_d, writes=[b_wE], chan="c0")
            hTb = sb(pe_, "hTb", [128, KC, 512], BF16)
            oab = sb(pe_, "oab", [128, NH, 512], BF16)
            obb = sb(pe_, "obb", [128, NH, 512], BF16)
            b_hTb, b_oab, b_obb = kb.buf("hTb"), kb.buf("oab"), kb.buf("obb")
            mT = sb(pe_, "mT", [128, KC, 512], BF16)
            b_mT = kb.buf("mT")
            sg = [sb(pe_, "sg%d" % i, [128, 512]) for i in range(4)]
            m12 = [sb(pe_, "m12_%d" % i, [128, 512]) for i in range(4)]
            b_sg, b_m12 = kb.bufs(4, "sg"), kb.bufs(4, "m12")
            xs_ = [sb(pe_, "xs%d" % i, [128, D]) for i in range(2)]
            b_xs = kb.bufs(2, "xs")
            yt = [sb(pe_, "yt%d" % i, [128, D]) for i in range(2)]
            b_yt = kb.bufs(2, "yt")
            junkE = sb(pe_, "junkE", [128, 512])
            b_junkE = kb.buf("junkE")
            ssE = sb(pe_, "ssE", [128, 4])
            b_ssE = kb.buf("ssE")
            nsub = 0
            for tb in range(NB):
                ts_ = slice(tb * 512, (tb + 1) * 512)
                kb.dma(hTb[:], hT_d[:, :, ts_], reads=[b_hTd[tb]], writes=[b_hTb], chan="ldx0")
                kb.dma(oab[:], oaT_d[:, :, ts_].rearrange("h p t -> p h t"), reads=[b_oaTd[h][tb] for h in range(NH)], writes=[b_oab], chan="ldx1")
                kb.dma(obb[:], obT_d[:, :, ts_].rearrange("h p t -> p h t"), reads=[b_obTd[h][tb] for h in range(NH)], writes=[b_obb], chan="ldx0")
                for dc in range(KC):
                    dsl = slice(dc * 128, (dc + 1) * 128)
                    par = dc % 2
                    for br, (wy, wgt, src, bsrc) in enumerate((("wa", "wga", oab, b_oab), ("wb", "wgb", obb, b_obb))):
                        by, bg = 2 * br + 4 * par, 2 * br + 1 + 4 * par
                        si = 2 * par + br
                        mm_group(kb, bank[by][:], [(wE[wy][:, e_, dsl], src[:, e_, :]) for e_ in range(NH)], [b_wE, bsrc], [b_bank[by]])
                        mm_group(kb, bank[bg][:], [(wE[wgt][:, kc, dsl], hTb[:, kc, :]) for kc in range(KC)], [b_wE, b_hTb], [b_bank[bg]])
                        kb.op("act", lambda e: e.activation(out=sg[si][:], in_=bank[bg][:], func=AF.Sigmoid), reads=[b_bank[bg]], writes=[b_sg[si]])
                        kb.op("dve", lambda e: e.tensor_tensor(out=m12[si][:], in0=bank[by][:], in1=sg[si][:], op=ALU.mult), reads=[b_bank[by], b_sg[si]], writes=[b_m12[si]])
                    kb.op("pool", lambda e: e.tensor_tensor(out=mT[:, dc, :], in0=m12[2 * par][:], in1=m12[2 * par + 1][:], op=ALU.add),
                          reads=[b_m12[2 * par], b_m12[2 * par + 1]], writes=[b_mT])
                for sub in range(4):
                    i = nsub % 2
                    nsub += 1
                    r0 = tb * 512 + sub * 128
                    kb.dma(xs_[i][:], x_d[r0:r0 + 128, :], writes=[b_xs[i]], chan="ldx%d" % i)
                    for hf in range(2):
                        bo = 4 + hf
                        mm_group(kb, bank[bo][:], [(mT[:, dc, sub * 128:(sub + 1) * 128], wE["wo"][:, dc, hf * 512:(hf + 1) * 512]) for dc in range(KC)],
                                 [b_mT, b_wE], [b_bank[bo]])
                        kb.op("act", lambda e: e.activation(out=junkE[:], in_=bank[bo][:], func=AF.Square, accum_out=ssE[:, hf:hf + 1]),
                              reads=[b_bank[bo]], writes=[b_junkE, b_ssE])
                    kb.op("dve", lambda e: e.tensor_tensor(out=ssE[:, 2:3], in0=ssE[:, 0:1], in1=ssE[:, 1:2], op=ALU.add), reads=[b_ssE], writes=[b_ssE])
                    kb.op("act", lambda e: e.activation(out=ssE[:, 2:3], in_=ssE[:, 2:3], func=AF.Sqrt, bias=EPS, scale=1.0 / D), reads=[b_ssE], writes=[b_ssE])
                    kb.op("dve", lambda e: e.reciprocal(out=ssE[:, 3:4], in_=ssE[:, 2:3]), reads=[b_ssE], writes=[b_ssE])
                    for hf in range(2):
                        bo = 4 + hf
                        hs = slice(hf * 512, (hf + 1) * 512)
                        kb.op("dve", lambda e: e.scalar_tensor_tensor(out=yt[i][:, hs], in0=bank[bo][:], scalar=ssE[:, 3:4], in1=pnw[:, hs], op0=ALU.mult, op1=ALU.mult),
                              reads=[b_bank[bo], b_ssE, b_wE], writes=[b_yt[i]])
                    kb.op("pool", lambda e: e.tensor_tensor(out=yt[i][:], in0=yt[i][:], in1=xs_[i][:], op=ALU.add), reads=[b_yt[i], b_xs[i]], writes=[b_yt[i]])
                    bo_ = kb.buf("yout")
                    kb.dma(y_d[r0:r0 + 128, :], yt[i][:], reads=[b_yt[i]], writes=[bo_], chan="sty%d" % i)
                    out_bufs.append(bo_)
        kb.finish(out_bufs)
        kb.barrier(("sp",))
        assert kb.check_deadlock()
    return nc


def host_inputs(inputs, b):
    m = {}
    w_in = inputs["w_in"][0]
    m["x"] = np.ascontiguousarray(inputs["x"][b])
    m["pnwT"] = np.ascontiguousarray(inputs["pre_norm_w"][0].reshape(KC, 128).T)
    m["idf"] = np.eye(128, dtype=np.float32)
    wm = np.zeros((NH, D, WM_COLS), np.float32)
    for h in range(NH):
        for g in range(4):
            wm[h, :, g * 128:(g + 1) * 128] = w_in[:, g * 1024 + h * 128: g * 1024 + (h + 1) * 128]
        for g, off in ((0, 512), (1, 544)):
            base = g * 1024 + h * 128
            wm[h, :, off:off + 16] = w_in[:, base + 16: base + 32]
            wm[h, :, off + 16:off + 32] = w_in[:, base: base + 16]
    m["wm"] = np.ascontiguousarray(wm.reshape(NH, KC, 128, WM_COLS).transpose(0, 2, 1, 3))
    half = 16
    inv_freq = np.power(np.float32(ROPE_THETA), -np.arange(half, dtype=np.float32) * np.float32(2.0 / 32)).astype(np.float32)
    ang = (np.arange(S, dtype=np.float32)[:, None] * inv_freq[None, :]).astype(np.float32)
    cos, sin = np.cos(ang).astype(np.float32).T, np.sin(ang).astype(np.float32).T
    m["ropecos"] = np.ascontiguousarray(np.concatenate([cos, cos], 0))
    m["ropesin"] = np.ascontiguousarray(np.concatenate([-sin, sin, -sin, sin], 0))
    m["utmask"] = np.triu(np.ones((128, 128), np.float32))
    m["negt"] = np.where(np.triu(np.ones((128, 128), bool)), 0.0, NEG).astype(np.float32)
    m["smask"] = np.triu(np.ones((128, 128), np.float32), k=1)
    wg = np.zeros((NH, D, 514), np.float32)
    for h in range(NH):
        for g in range(4):
            wg[h, :, g * 128:(g + 1) * 128] = w_in[:, 4096 + g * 1024 + h * 128: 4096 + g * 1024 + (h + 1) * 128]
        wg[h, :, 512] = w_in[:, 8192 + h]
        wg[h, :, 513] = w_in[:, 8200 + h]
    m["wg"] = np.ascontiguousarray(wg.reshape(NH, KC, 128, 514).transpose(0, 2, 1, 3))
    m["cw"] = np.ascontiguousarray(inputs["conv_w"][0].reshape(4, 24, 128).transpose(2, 1, 0))
    m["dtb"] = np.ascontiguousarray(np.broadcast_to(inputs["dt_bias"][0][None, :], (128, NH)))
    m["alog"] = np.ascontiguousarray(np.broadcast_to(inputs["a_log"][0][None, :], (128, NH)))
    m["gnw"] = np.ascontiguousarray(inputs["gdn_norm_w"][0].reshape(128, 1))
    lay = lambda w_: np.ascontiguousarray(w_.reshape(KC, 128, D).transpose(1, 0, 2))
    m["wa"] = lay(inputs["w_branch_a"][0])
    m["wb"] = lay(inputs["w_branch_b"][0])
    m["wo"] = lay(inputs["w_out"][0])
    m["wga"] = lay(w_in[:, 8208:9232])
    m["wgb"] = lay(w_in[:, 9232:10256])
    m["pnw"] = np.ascontiguousarray(np.broadcast_to(inputs["post_norm_w"][0][None, :], (128, D)))
    return m


def kernel(**inputs):
    inputs = {k: np.asarray(v) for k, v in inputs.items()}
    nc = build_program(DEBUG)
    in_maps = [host_inputs(inputs, b) for b in range(8)]
    res = run_bass_kernel_spmd(nc, in_maps, core_ids=list(range(8)))
    if DEBUG.get("raw"):
        return res
    return np.stack([r["y"] for r in res.results], axis=0).astype(np.float32)
```
